# Optimizing a Trainium2 kernel written in Bass

```python
import jax
import jax.numpy as jnp
from jax import lax
import numpy as np

D_MODEL = 1024
BATCH = 16
SEQ = 256
DEPTH = 4
DEC_BATCH = 8
DEC_SEQ = 2048
PAST_LEN = 512

GRID_W = 64
N_MIXERS = 3
N_CONV_LAYERS = (DEPTH + 2) // N_MIXERS
N_POOL_LAYERS = (DEPTH + 1) // N_MIXERS
N_ATTN_LAYERS = DEPTH // N_MIXERS
N_SUBLAYERS = 3
N_MOD = 3 * N_SUBLAYERS
D_FF = 2816
N_HEADS = 16
N_KV_HEADS = 4
HEAD_DIM = D_MODEL // N_HEADS
GQA_GROUP = N_HEADS // N_KV_HEADS
QKV_DIM = (N_HEADS + 2 * N_KV_HEADS) * HEAD_DIM
WINDOW = 128
BLOCK = 128
SPAN = BLOCK + 2 * WINDOW
ROPE_THETA = 10000.0
CONV_WIDTH = 3
POOL_WINDOWS = (2, 4, 8, 16)
N_POOL_GROUPS = len(POOL_WINDOWS)
POOL_GROUP_DIM = D_MODEL // N_POOL_GROUPS
DEEPNORM_ALPHA = (2.0 * DEPTH) ** 0.25
DEEPNORM_BETA = (8.0 * DEPTH) ** -0.25
LN_EPS = 1e-5
NEG_INF = -1e30
ATTN_SCALE = HEAD_DIM ** -0.5

kernel_name = 'hybrid_flow_prefix_trunk_step'


def layer_norm(x, g, b):
    xf = x.astype(jnp.float32)
    mu = jnp.mean(xf, axis=-1, keepdims=True)
    var = jnp.mean(jnp.square(xf - mu), axis=-1, keepdims=True)
    y = (xf - mu) * lax.rsqrt(var + LN_EPS)
    return (y * g.astype(jnp.float32) + b.astype(jnp.float32)).astype(x.dtype)


def modulation(cvec, w_mod, b_mod):
    m = jax.nn.silu(cvec) @ w_mod + b_mod
    return m.reshape(cvec.shape[0], N_MOD, 1, D_MODEL)


def modulate(x, mod, j):
    return x * (1.0 + mod[:, 3 * j + 1]) + mod[:, 3 * j]


def post_residual(x, out, mod, j, g, b):
    return layer_norm(DEEPNORM_ALPHA * x + mod[:, 3 * j + 2] * out, g, b)


def swiglu(h, w_gate_up, w_down):
    g, u = jnp.split(h @ w_gate_up, 2, axis=-1)
    return (jax.nn.silu(g) * u) @ w_down


def short_conv_mixer(h, w_in, conv_k, w_out):
    L = h.shape[1]
    b_gate, c_gate, v = jnp.split(h @ w_in, 3, axis=-1)
    u = c_gate * v
    pad = CONV_WIDTH // 2
    up = jnp.pad(u, ((0, 0), (pad, pad), (0, 0)))
    conv = up[:, 0:L] * conv_k[0]
    for tap in range(1, CONV_WIDTH):
        conv = conv + up[:, tap:tap + L] * conv_k[tap]
    return (b_gate * conv) @ w_out


def pool_mixer(h, pool_w, pool_scale):
    B, L, D = h.shape
    hf = h.astype(jnp.float32)
    cs = jnp.concatenate([jnp.zeros((B, 1, D), jnp.float32), lax.cumsum(hf, axis=1)], axis=1)
    t = np.arange(L)
    outs = []
    for g, w in enumerate(POOL_WINDOWS):
        lo = np.clip(t - w // 2, 0, L)
        hi = np.clip(t + w // 2, 0, L)
        sl = slice(g * POOL_GROUP_DIM, (g + 1) * POOL_GROUP_DIM)
        csg = cs[:, :, sl]
        cnt = jnp.asarray((hi - lo)[None, :, None], jnp.float32)
        outs.append((csg[:, hi] - csg[:, lo]) / cnt - hf[:, :, sl])
    d = jnp.stack(outs, axis=2).astype(h.dtype)
    y = jnp.einsum('blgc,gcd->blgd', d, pool_w).reshape(B, L, D)
    return y * pool_scale


def axial_rope_tables(n_rows):
    n_freq = HEAD_DIM // 4
    inv_freq = ROPE_THETA ** (-jnp.arange(n_freq, dtype=jnp.float32) / n_freq)
    t = jnp.arange(n_rows * GRID_W)
    row = (t // GRID_W).astype(jnp.float32)
    col = (t % GRID_W).astype(jnp.float32)
    ang = jnp.concatenate([row[:, None] * inv_freq, col[:, None] * inv_freq], axis=-1)
    return jnp.cos(ang), jnp.sin(ang)


def apply_rope(x, cos, sin):
    xf = x.astype(jnp.float32)
    x1, x2 = jnp.split(xf, 2, axis=-1)
    shape = (1, cos.shape[0]) + (1,) * (x.ndim - 3) + (cos.shape[1],)
    cos = cos.reshape(shape)
    sin = sin.reshape(shape)
    return jnp.concatenate([x1 * cos - x2 * sin, x1 * sin + x2 * cos], axis=-1).astype(x.dtype)


def attn_project(h, w_qkv):
    B, L, _ = h.shape
    qkv = h @ w_qkv
    nq = N_HEADS * HEAD_DIM
    nk = N_KV_HEADS * HEAD_DIM
    q = qkv[..., :nq].reshape(B, L, N_KV_HEADS, GQA_GROUP, HEAD_DIM)
    k = qkv[..., nq:nq + nk].reshape(B, L, N_KV_HEADS, HEAD_DIM)
    v = qkv[..., nq + nk:].reshape(B, L, N_KV_HEADS, HEAD_DIM)
    return q, k, v


def sink_softmax(s, sink):
    sk = sink.astype(jnp.float32)[:, :, None, None]
    m = jnp.maximum(jnp.max(s, axis=-1, keepdims=True), sk)
    p = jnp.exp(s - m)
    return p / (jnp.sum(p, axis=-1, keepdims=True) + jnp.exp(sk - m))


def context_attention(q, k, v, sink):
    B, S = q.shape[0], q.shape[1]
    nb = S // BLOCK
    qb = q.reshape(B, nb, BLOCK, N_KV_HEADS, GQA_GROUP, HEAD_DIM).swapaxes(0, 1)

    def one(qblk):
        s = jnp.einsum('bqhgd,bkhd->bhgqk', qblk, k, preferred_element_type=jnp.float32) * ATTN_SCALE
        p = sink_softmax(s, sink).astype(v.dtype)
        return jnp.einsum('bhgqk,bkhd->bqhgd', p, v)

    o = lax.map(one, qb)
    return o.swapaxes(0, 1).reshape(B, S, D_MODEL)


def latent_attention(q, k, v, ck, cv, sink):
    B, L = q.shape[0], q.shape[1]
    nb = L // BLOCK
    kp = jnp.pad(k, ((0, 0), (WINDOW, WINDOW), (0, 0), (0, 0)))
    vp = jnp.pad(v, ((0, 0), (WINDOW, WINDOW), (0, 0), (0, 0)))
    qb = q.reshape(B, nb, BLOCK, N_KV_HEADS, GQA_GROUP, HEAD_DIM).swapaxes(0, 1)
    starts = jnp.arange(nb, dtype=jnp.int32) * BLOCK
    kpos_rel = np.arange(SPAN) - WINDOW
    band = jnp.asarray(np.abs(kpos_rel[None, :] - np.arange(BLOCK)[:, None]) <= WINDOW)
    kpos_rel_j = jnp.asarray(kpos_rel, jnp.int32)

    def one(args):
        qblk, start = args
        kb = lax.dynamic_slice_in_dim(kp, start, SPAN, axis=1)
        vb = lax.dynamic_slice_in_dim(vp, start, SPAN, axis=1)
        kabs = start + kpos_rel_j
        valid = band & ((kabs >= 0) & (kabs < L))[None, :]
        s_band = jnp.einsum('bqhgd,bkhd->bhgqk', qblk, kb, preferred_element_type=jnp.float32) * ATTN_SCALE
        s_band = jnp.where(valid, s_band, NEG_INF)
        s_ctx = jnp.einsum('bqhgd,bkhd->bhgqk', qblk, ck, preferred_element_type=jnp.float32) * ATTN_SCALE
        p = sink_softmax(jnp.concatenate([s_band, s_ctx], axis=-1), sink).astype(v.dtype)
        return (jnp.einsum('bhgqk,bkhd->bqhgd', p[..., :SPAN], vb)
                + jnp.einsum('bhgqk,bkhd->bqhgd', p[..., SPAN:], cv))

    o = lax.map(one, (qb, starts))
    return o.swapaxes(0, 1).reshape(B, L, D_MODEL)


def setup_inputs(seed: int = 0) -> dict:
    key = jax.random.key(seed)
    ks = jax.random.split(key, 20)
    D = D_MODEL

    def nrm(k, shape, s=1.0):
        return jax.random.normal(k, shape, jnp.float32) * s

    return {
        'x_prompt': nrm(ks[0], (BATCH, SEQ, D)),
        'x_sample': nrm(ks[1], (DEC_BATCH, DEC_SEQ, D)),
        'cache_ctx_k': nrm(ks[2], (DEC_BATCH, N_ATTN_LAYERS, PAST_LEN, N_KV_HEADS, HEAD_DIM)),
        'cache_ctx_v': nrm(ks[3], (DEC_BATCH, N_ATTN_LAYERS, PAST_LEN, N_KV_HEADS, HEAD_DIM)),
        'c': nrm(ks[4], (DEC_BATCH, D)),
        'c_ctx': nrm(ks[5], (D,)),
        'w_mod': nrm(ks[6], (DEPTH, D, N_MOD * D), D ** -0.5),
        'b_mod': nrm(ks[7], (DEPTH, N_MOD * D), 0.01),
        'ln_g': 1.0 + nrm(ks[8], (DEPTH, N_SUBLAYERS, D), 0.02),
        'ln_b': nrm(ks[9], (DEPTH, N_SUBLAYERS, D), 0.02),
        'ffn_w_gate_up': nrm(ks[10], (DEPTH, 2, D, 2 * D_FF), D ** -0.5),
        'ffn_w_down': nrm(ks[11], (DEPTH, 2, D_FF, D), D_FF ** -0.5 * DEEPNORM_BETA),
        'conv_w_in': nrm(ks[12], (N_CONV_LAYERS, D, 3 * D), D ** -0.5),
        'conv_k': nrm(ks[13], (N_CONV_LAYERS, CONV_WIDTH, D), CONV_WIDTH ** -0.5),
        'conv_w_out': nrm(ks[14], (N_CONV_LAYERS, D, D), D ** -0.5 * DEEPNORM_BETA),
        'pool_w': nrm(ks[15], (N_POOL_LAYERS, N_POOL_GROUPS, POOL_GROUP_DIM, POOL_GROUP_DIM),
                      POOL_GROUP_DIM ** -0.5 * DEEPNORM_BETA),
        'pool_scale': 1.0 + nrm(ks[16], (N_POOL_LAYERS, D), 0.02),
        'attn_w_qkv': nrm(ks[17], (N_ATTN_LAYERS, D, QKV_DIM), D ** -0.5),
        'attn_w_o': nrm(ks[18], (N_ATTN_LAYERS, D, D), D ** -0.5 * DEEPNORM_BETA),
        'attn_sink': nrm(ks[19], (N_ATTN_LAYERS, N_HEADS), 0.5),
    }


def reference(x_prompt, x_sample, cache_ctx_k, cache_ctx_v, c, c_ctx, w_mod, b_mod, ln_g, ln_b,
              ffn_w_gate_up, ffn_w_down, conv_w_in, conv_k, conv_w_out, pool_w, pool_scale,
              attn_w_qkv, attn_w_o, attn_sink):
    n_rows = x_sample.shape[1] // GRID_W
    cos, sin = axial_rope_tables(n_rows)
    y_p, y_s = x_prompt, x_sample
    ctx_k_list, ctx_v_list = [], []
    for i in range(DEPTH):
        mod_p = modulation(c_ctx[None, :], w_mod[i], b_mod[i])
        mod_s = modulation(c, w_mod[i], b_mod[i])

        w_gu, w_dn = ffn_w_gate_up[i, 0], ffn_w_down[i, 0]
        y_p = post_residual(y_p, 0.5 * swiglu(modulate(y_p, mod_p, 0), w_gu, w_dn), mod_p, 0, ln_g[i, 0], ln_b[i, 0])
        y_s = post_residual(y_s, 0.5 * swiglu(modulate(y_s, mod_s, 0), w_gu, w_dn), mod_s, 0, ln_g[i, 0], ln_b[i, 0])

        kind, j = i % N_MIXERS, i // N_MIXERS
        h_p = modulate(y_p, mod_p, 1)
        h_s = modulate(y_s, mod_s, 1)
        if kind == 0:
            o_p = short_conv_mixer(h_p, conv_w_in[j], conv_k[j], conv_w_out[j])
            o_s = short_conv_mixer(h_s, conv_w_in[j], conv_k[j], conv_w_out[j])
        elif kind == 1:
            o_p = pool_mixer(h_p, pool_w[j], pool_scale[j])
            o_s = pool_mixer(h_s, pool_w[j], pool_scale[j])
        else:
            sink = attn_sink[j].reshape(N_KV_HEADS, GQA_GROUP)
            q_p, k_p, v_p = attn_project(h_p, attn_w_qkv[j])
            ctx_k_list.append(k_p)
            ctx_v_list.append(v_p)
            o_p = context_attention(q_p, k_p, v_p, sink) @ attn_w_o[j]
            q_s, k_s, v_s = attn_project(h_s, attn_w_qkv[j])
            q_s = apply_rope(q_s, cos, sin)
            k_s = apply_rope(k_s, cos, sin)
            o_s = latent_attention(q_s, k_s, v_s, cache_ctx_k[:, j], cache_ctx_v[:, j], sink) @ attn_w_o[j]
        y_p = post_residual(y_p, o_p, mod_p, 1, ln_g[i, 1], ln_b[i, 1])
        y_s = post_residual(y_s, o_s, mod_s, 1, ln_g[i, 1], ln_b[i, 1])

        w_gu, w_dn = ffn_w_gate_up[i, 1], ffn_w_down[i, 1]
        y_p = post_residual(y_p, 0.5 * swiglu(modulate(y_p, mod_p, 2), w_gu, w_dn), mod_p, 2, ln_g[i, 2], ln_b[i, 2])
        y_s = post_residual(y_s, 0.5 * swiglu(modulate(y_s, mod_s, 2), w_gu, w_dn), mod_s, 2, ln_g[i, 2], ln_b[i, 2])

    ctx_k = jnp.stack(ctx_k_list, axis=1)
    ctx_v = jnp.stack(ctx_v_list, axis=1)
    return (y_p, y_s, ctx_k, ctx_v)
```

```python
import numpy as np
import concourse.bass as bass
import concourse.mybir as mybir
from concourse.bass_utils import run_bass_kernel_spmd

F32 = mybir.dt.float32
BF16 = mybir.dt.bfloat16
AF = mybir.ActivationFunctionType
ALU = mybir.AluOpType

D = 1024
KC = 8
DEPTH = 4
DFF = 2816
NJ = DFF // 128
NTOK = 2560
TS = 512
NT = NTOK // TS
LS = 2048
LP = 256
NHEAD = 16
HD = 64
PAST = 512
ALPHA = (2.0 * DEPTH) ** 0.25
LN_EPS = 1e-5
EPS_P = LN_EPS / (ALPHA * ALPHA)
ATT_SCALE = HD ** -0.5
SEGS = ((0, 2048), (2048, 2304), (2304, 2560))
NCORES = 8

PE, ACT, DVE, POOL, SP = "pe", "act", "dve", "pool", "sp"
ENGS = (PE, ACT, DVE, POOL, SP)


class _Op:
    __slots__ = ("fn", "deps", "sig", "cnt", "dma", "tag")

    def __init__(self, fn, deps, dma=None):
        self.tag = None
        self.fn = fn
        self.deps = deps
        self.sig = False
        self.cnt = 0
        self.dma = dma


class Prog:
    def __init__(self):
        self.ops = {e: [] for e in ENGS}
        self.res = {}
        self.dma_cnt = {}
        self.tag = ""

    def _collect(self, reads, writes, me, idx):
        deps = {}
        res = self.res
        for k in reads:
            st = res.get(k)
            if st is None:
                st = res[k] = [{}, {}]
            for p, i in st[0].items():
                if deps.get(p, -1) < i:
                    deps[p] = i
        for k in writes:
            st = res.get(k)
            if st is None:
                st = res[k] = [{}, {}]
            for p, i in st[0].items():
                if deps.get(p, -1) < i:
                    deps[p] = i
            for p, i in st[1].items():
                if deps.get(p, -1) < i:
                    deps[p] = i
        for k in reads:
            st = res[k]
            if st[1].get(me, -1) < idx:
                st[1][me] = idx
        for k in writes:
            res[k] = [{me: idx}, {}]
        return deps

    def op(self, eng, fn, reads=(), writes=()):
        idx = len(self.ops[eng])
        deps = self._collect(reads, writes, eng, idx)
        o = _Op(fn, deps)
        o.tag = self.tag
        self.ops[eng].append(o)

    def dma(self, queue, slot, out, in_, reads=(), writes=()):
        n = self.dma_cnt.get(slot, 0) + 1
        self.dma_cnt[slot] = n
        me = ("dma", slot)
        deps = self._collect(reads, writes, me, n)
        deps.pop(me, None)
        self.ops[queue].append(_Op(lambda e, o=out, i=in_: e.dma_start(out=o, in_=i), deps, dma=(slot, n)))

    def wait_all_dma(self, queue, slots):
        deps = {("dma", s): self.dma_cnt[s] for s in slots if self.dma_cnt.get(s)}
        self.ops[queue].append(_Op(None, deps))

    def prepare(self, nc):
        for e in ENGS:
            for o in self.ops[e]:
                for p, i in o.deps.items():
                    if isinstance(p, str):
                        if p == e and e == PE:
                            continue
                        self.ops[p][i].sig = True
        for e in ENGS:
            c = 0
            for o in self.ops[e]:
                if o.sig:
                    c += 1
                o.cnt = c
        sems = {}
        import contextlib
        stack = contextlib.ExitStack()
        for e in ENGS:
            sems[e] = stack.enter_context(nc.semaphore("sem_" + e))
        for s in self.dma_cnt:
            sems[("dma", s)] = stack.enter_context(nc.semaphore("dsem_" + s))
        self._stack = stack
        self._sems = sems

    def emit(self, nc, block):
        sems = self._sems

        def run(e, eng):
            seen = {}
            for o in self.ops[e]:
                for p, i in o.deps.items():
                    if isinstance(p, str):
                        if p == e and e == PE:
                            continue
                        need = self.ops[p][i].cnt
                    else:
                        need = 16 * i
                    if seen.get(p, 0) < need:
                        eng.wait_ge(sems[p], need)
                        seen[p] = need
                if o.fn is None:
                    continue
                ins = o.fn(eng)
                if o.dma is not None:
                    ins.then_inc(sems[("dma", o.dma[0])], 16)
                elif o.sig:
                    ins.then_inc(sems[e], 1)

        block.tensor(lambda eng: run(PE, eng))
        block.scalar(lambda eng: run(ACT, eng))
        block.vector(lambda eng: run(DVE, eng))
        block.gpsimd(lambda eng: run(POOL, eng))
        block.sync(lambda eng: run(SP, eng))


def _fm(v):
    v = np.asarray(v, np.float32)
    lead = v.shape[:-1]
    r = v.reshape(lead + (KC, 128))
    r = np.moveaxis(r, -1, 0)
    return np.ascontiguousarray(r)


def _swap32(w):
    s = w.shape
    r = w.reshape(s[:-1] + (s[-1] // 64, 2, 32))
    return r[..., ::-1, :].reshape(s)


def _rope_tables():
    n_freq = HD // 4
    inv = (10000.0 ** (-(np.arange(n_freq, dtype=np.float32) / np.float32(n_freq)))).astype(np.float32)
    t = np.arange(LS)
    row = (t // 64).astype(np.float32)
    col = (t % 64).astype(np.float32)
    ang = np.concatenate([row[:, None] * inv, col[:, None] * inv], axis=-1).astype(np.float32)
    cos = np.cos(ang).astype(np.float32).T
    sin = np.sin(ang).astype(np.float32).T
    p = np.arange(128)
    cosT = cos[p % 32]
    sgn = np.where((p % 64) < 32, -1.0, 1.0).astype(np.float32)[:, None]
    sinT = sin[p % 32] * sgn
    return np.ascontiguousarray(cosT), np.ascontiguousarray(sinT)


def _pool_inv_counts():
    out = np.zeros((128, 4, 2, 8), np.float32)
    L = 256
    for g, w in enumerate((2, 4, 8, 16)):
        t = np.arange(L)
        lo = np.clip(t - w // 2, 0, L)
        hi = np.clip(t + w // 2, 0, L)
        inv = (1.0 / (hi - lo).astype(np.float32)).astype(np.float32)
        out[:, g, 0, :] = inv[:8]
        out[:, g, 1, :] = inv[L - 8:]
    return out


class Builder:
    def __init__(self, n_layers=DEPTH, mixers=True):
        self.n_layers = n_layers
        self.mixers = mixers
        self.nc = bass.Bass("TRN2", target_bir_lowering=False)
        self.P = Prog()
        self.psum_i = 0
        self.w1_i = 0
        self.w2_i = 0
        self.tmp_i = {}
        self.ln_pending = []
        self.ln_deferred = []
        self.next_sub = None
        self.next_head_state = None
        self.sgi = 0

    def declare(self):
        nc = self.nc
        di = lambda name, shape: nc.dram_tensor(name, list(shape), F32, kind="ExternalInput").ap()
        do = lambda name, shape: nc.dram_tensor(name, list(shape), F32, kind="ExternalOutput").ap()
        self.d_xT = di("xT", (D, NTOK))
        self.d_cT = di("cT", (128, KC, 2))
        self.d_wmod = di("w_mod", (DEPTH, D, 9 * D))
        self.d_bmodT = di("bmodT", (128, DEPTH, 72))
        self.d_lnT = di("lnT", (128, DEPTH, 3, 2, KC))
        self.d_wgu = di("w_gu", (DEPTH, 2, D, 2 * DFF))
        self.d_wdn = di("w_dn", (DEPTH, 2, DFF, D))
        self.d_cwin = di("conv_w_in", (2, D, 3 * D))
        self.d_ckT = di("convkT", (128, 2, 3, KC))
        self.d_cwout = di("conv_w_out", (2, D, D))
        self.d_poolw = di("pool_w", (4, 256, 256))
        self.d_poolsT = di("poolsT", (128, KC))
        self.d_iedge = di("iedge", (128, 4, 2, 8))
        self.d_wqk = di("wqk", (4, D, 832))
        self.d_wkv = di("wkv", (D, 512))
        self.d_wo = di("w_o", (D, D))
        self.d_sink = di("sinkp", (1, 16))
        self.d_cos = di("cosT", (128, LS))
        self.d_sin = di("sinT", (128, LS))
        self.d_tri = di("trim", (128, 2, 256))
        self.d_ident = di("ident", (128, 128))
        self.d_kctx = di("kctxT", (4, 128, PAST))
        self.d_vctx = di("vctx", (128, 4, 4, HD))
        self.o_yT = do("yT", (D, NTOK))
        self.o_kv = do("kvout", (2 * LP, 512))

        sb = lambda name, shape, dt: nc.alloc_sbuf_tensor(name, list(shape), dt)
        self.x = sb("x", (128, KC, NTOK), F32)
        self.h = sb("h", (128, KC, NTOK), BF16)
        self.abuf = sb("abuf", (128, 4, NTOK), BF16)
        self.w1 = sb("w1", (128, 3, KC * 512), BF16)
        self.w2 = sb("w2", (128, 2, 2 * D), BF16)
        self.SCRW = 6656
        self.scr = sb("scr", (128, self.SCRW), F32)
        self.cT = sb("cTs", (128, KC, 2), F32)
        self.cTb = sb("cTb", (128, KC, 2), BF16)
        self.bmodT = sb("bmodTs", (128, DEPTH, 72), F32)
        self.lnT = sb("lnTs", (128, DEPTH, 3, 2, KC), F32)
        self.ckT = sb("ckTs", (128, 2, 3, KC), F32)
        self.poolsT = sb("poolsTs", (128, KC), F32)
        self.modT = sb("modT", (128, 2, 9, KC, 2), F32)
        self.der = sb("der", (128, 2, 8, KC, 2), F32)
        self.ones = sb("ones", (128, 128), BF16)
        self.iedge = sb("iedges", (128, 4, 2, 8), F32)
        self.trim = sb("trims", (128, 2, 256), BF16)
        self.ident = sb("idents", (128, 128), BF16)
        self.ones1 = sb("ones1", (128, 128), BF16)
        self.es = sb("es", (1, 16), F32)
        self.ps = [nc.alloc_psum_tensor("ps%d" % i, [128, 512], F32) for i in range(8)]

    def bank(self):
        b = self.psum_i
        self.psum_i = (b + 1) % 8
        return b

    def sview(self, off_words, shape, dt):
        n = int(np.prod(shape))
        words = n if dt == F32 else (n + 1) // 2
        assert off_words + words <= self.SCRW, (off_words, words)
        ap = self.scr[:, off_words:off_words + words]
        if dt != F32:
            ap = ap.bitcast(dt)
        if len(shape) == 2:
            ap = ap.rearrange("p (a b) -> p a b", a=shape[0])
        elif len(shape) == 3:
            ap = ap.rearrange("p (a b c) -> p a b c", a=shape[0], b=shape[1])
        keys = [("scr", c) for c in range(off_words // 128, (off_words + words + 127) // 128)]
        return ap, keys

    def load_w1(self, parts):
        s = self.w1_i
        self.w1_i = (s + 1) % 3
        off = 0
        offs = []
        for ap, ncols in parts:
            dst = self.w1[:, s, :].rearrange("p (k n) -> p k n", k=KC)[:, :, off:off + ncols]
            self.P.dma(POOL, "w1_%d" % s, dst, ap, writes=[("w1", s)])
            offs.append(off)
            off += ncols
        assert off <= 512
        return s, offs

    def w1v(self, s, k, c0, n):
        return self.w1[:, s, :].rearrange("p (k n) -> p k n", k=KC)[:, k, c0:c0 + n]

    def load_w2(self, ap, nrow_chunks, ncols):
        s = self.w2_i
        self.w2_i = (s + 1) % 2
        dst = self.w2[:, s, 0:nrow_chunks * ncols].rearrange("p (c n) -> p c n", c=nrow_chunks)
        self.P.dma(POOL, "w2_%d" % s, dst, ap, writes=[("w2", s)])
        return s

    def w2v(self, s, c, ncols, c0, n):
        return self.w2[:, s, c * ncols + c0: c * ncols + c0 + n]

    def prologue(self):
        P, nc = self.P, self.nc
        small = [(self.cT, self.d_cT, "cT"), (self.bmodT, self.d_bmodT, "bmodT"), (self.lnT, self.d_lnT, "lnT"),
                 (self.ckT, self.d_ckT, "ckT"), (self.poolsT, self.d_poolsT, "poolsT"),
                 (self.iedge, self.d_iedge, "iedge")]
        for t, d, name in small:
            P.dma(SP, "sm_" + name, t[:], d, writes=[name])
        P.op(DVE, lambda e: e.memset(self.ones[:], 1.0 / D), writes=["ones"])
        P.op(ACT, lambda e: e.activation(out=self.cTb[:], in_=self.cT[:], func=AF.Silu), reads=["cT"], writes=["cTb"])
        xv = self.d_xT.rearrange("(k p) t -> p k t", p=128)
        for t in range(NT):
            P.dma(SP, "xin%d" % t, self.x[:, :, t * TS:(t + 1) * TS], xv[:, :, t * TS:(t + 1) * TS],
                  writes=[("x", k, t) for k in range(KC)])

    def compute_mod(self, l):
        self.mod_pending = list(getattr(self, "mod_pending", [])) + [(l, pc) for pc in range(18)]

    def mod_prefetch(self):
        if getattr(self, "mod_pref", None) is not None or not getattr(self, "mod_pending", None):
            return
        l, pc = self.mod_pending.pop(0)
        wv = self.d_wmod[l].rearrange("(k p) n -> p k n", p=128)
        s, _ = self.load_w1([(wv[:, :, pc * 512:(pc + 1) * 512], 512)])
        self.mod_pref = (l, pc, s)

    def mod_step(self, n=1):
        P = self.P
        for _ in range(n):
            if getattr(self, "mod_pref", None) is None:
                self.mod_prefetch()
            if getattr(self, "mod_pref", None) is None:
                return
            l, pc, s = self.mod_pref
            self.mod_pref = None
            par = l % 2
            b = self.bank()
            pst = self.ps[b]
            for oc in range(4):
                for k in range(KC):
                    P.op(PE, lambda e, s=s, k=k, oc=oc, pst=pst: e.matmul(
                        pst[:, oc * 2:oc * 2 + 2], self.w1v(s, k, oc * 128, 128), self.cTb[:, k, :],
                        start=(k == 0), stop=(k == KC - 1)),
                        reads=[("w1", s), "cTb"], writes=[("ps", b)])
            dst = self.modT[:, par].rearrange("p m k r -> p (m k) r")[:, pc * 4:(pc + 1) * 4, :]
            src = pst[:, 0:8].rearrange("p (c r) -> p c r", r=2)
            bm = self.bmodT[:, l, pc * 4:(pc + 1) * 4].unsqueeze(2).broadcast_to([128, 4, 2])
            P.op(DVE, lambda e, dst=dst, src=src, bm=bm: e.tensor_tensor(out=dst, in0=src, in1=bm, op=ALU.add),
                 reads=[("ps", b), "bmodT"], writes=[("mod", par)] + [("gq", par, jx) for jx in range(3)])

    def derive(self, l, js=(0, 1, 2)):
        P = self.P
        par = l % 2
        mod = self.modT[:, par]
        der = self.der[:, par]
        rd = [("mod", par), "lnT"]
        for j in js:
            sc = mod[:, 3 * j + 1]
            sh = mod[:, 3 * j]
            gt = mod[:, 3 * j + 2]
            coef = (0.5 if j != 1 else 1.0) / ALPHA
            if j == 1:
                P.op(DVE, lambda e, sc=sc: e.tensor_scalar(out=der[:, 6], in0=sc, scalar1=1.0, scalar2=None, op0=ALU.add),
                     reads=rd, writes=[("der", par, 6)])
            if l == 0 and j == 0:
                P.op(DVE, lambda e, sc=sc: e.tensor_scalar(out=der[:, 0], in0=sc, scalar1=1.0, scalar2=None, op0=ALU.add),
                     reads=rd, writes=[("der", par, 0)])
                P.op(DVE, lambda e, sh=sh: e.tensor_copy(out=der[:, 3], in_=sh), reads=rd, writes=[("der", par, 3)])
            else:
                pl, pj = (l, j - 1) if j > 0 else (l - 1, 2)
                gam = self.lnT[:, pl, pj, 0, :].unsqueeze(2).broadcast_to([128, KC, 2])
                bet = self.lnT[:, pl, pj, 1, :].unsqueeze(2).broadcast_to([128, KC, 2])
                P.op(DVE, lambda e, sc=sc, gam=gam, j=j: e.scalar_tensor_tensor(
                    out=der[:, j], in0=sc, scalar=1.0, in1=gam, op0=ALU.add, op1=ALU.mult),
                    reads=rd, writes=[("der", par, j)])
                P.op(DVE, lambda e, sc=sc, bet=bet, j=j: e.scalar_tensor_tensor(
                    out=der[:, 3 + j], in0=sc, scalar=1.0, in1=bet, op0=ALU.add, op1=ALU.mult),
                    reads=rd, writes=[("der", par, 3 + j)])
                P.op(DVE, lambda e, sh=sh, j=j: e.tensor_tensor(out=der[:, 3 + j], in0=der[:, 3 + j], in1=sh, op=ALU.add),
                     reads=rd + [("der", par, 3 + j)], writes=[("der", par, 3 + j)])
            P.op(DVE, lambda e, gt=gt, coef=coef: e.tensor_scalar(out=gt, in0=gt, scalar1=coef, scalar2=None, op0=ALU.mult),
                 reads=rd, writes=[("gq", par, j)])
        if l == 1 and self.mixers and 1 in js:
            gt = mod[:, 5]
            ps_b = self.poolsT[:].unsqueeze(2).broadcast_to([128, KC, 2])
            P.op(DVE, lambda e: e.tensor_tensor(out=gt, in0=gt, in1=ps_b, op=ALU.mult),
                 reads=[("gq", par, 1), "poolsT"], writes=[("gq", par, 1)])

    def modulate_first(self):
        P = self.P
        for t in range(NT):
            r = 0 if t < 4 else 1
            for k in range(KC):
                P.op(ACT, lambda e, k=k, t=t, r=r: e.activation(
                    out=self.h[:, k, t * TS:(t + 1) * TS], in_=self.x[:, k, t * TS:(t + 1) * TS], func=AF.Identity,
                    scale=self.der[:, 0, 0, k, r:r + 1], bias=self.der[:, 0, 3, k, r:r + 1]),
                    reads=[("x", k, t), ("der", 0, 0), ("der", 0, 3)], writes=[("h", k, t)])

    def phase2(self, l, j, w2slots, n_in, out_chunks, ncols=D, col_of=None, after_tile=None, before_tile=None):
        P = self.P
        par = l % 2
        for t in range(NT):
            r = 0 if t < 4 else 1
            if before_tile is not None:
                before_tile(t)
            for o in out_chunks:
                c0 = (o * 128) if col_of is None else col_of(o)
                b = self.bank()
                first = True
                nmm = sum(n for (_, _, n) in w2slots)
                i = 0
                for (s, a0, n) in w2slots:
                    for c in range(n):
                        i += 1
                        P.op(PE, lambda e, s=s, c=c, a=a0 + c, t=t, b=b, c0=c0, st=first, sp=(i == nmm): e.matmul(
                            self.ps[b][:, :], self.w2v(s, c, ncols, c0, 128), self.abuf[:, a, t * TS:(t + 1) * TS],
                            start=st, stop=sp),
                            reads=[("w2", s), ("a", a0 + c, t)], writes=[("ps", b)])
                        first = False
                xs = self.x[:, o, t * TS:(t + 1) * TS]
                P.op(DVE, lambda e, b=b, xs=xs, o=o, r=r: e.scalar_tensor_tensor(
                    out=xs, in0=self.ps[b][:, :], scalar=self.modT[:, par, 3 * j + 2, o, r:r + 1], in1=xs,
                    op0=ALU.mult, op1=ALU.add),
                    reads=[("ps", b), ("gq", par, j), ("x", o, t)], writes=[("x", o, t)])
            if after_tile is not None:
                after_tile(t)

    def ln_views(self):
        zb, kzb = self.sview(0, (KC, TS), BF16)
        zq, kzq = self.sview(2048, (KC, TS), BF16)
        rs, krs = self.sview(4096, (TS,), F32)
        mn, kmn = self.sview(4608, (TS,), F32)
        return zb, kzb, zq, kzq, rs, krs, mn, kmn

    def ln_stage(self, l, j, last, stage, t):
        P = self.P
        zb, kzb, zq, kzq, rs, krs, mn, kmn = self.ln_views()
        if j < 2:
            nl, nj = l, j + 1
        else:
            nl, nj = l + 1, 0
        npar = nl % 2
        r = 0 if t < 4 else 1
        xt = self.x[:, :, t * TS:(t + 1) * TS]
        xk = [("x", k, t) for k in range(KC)]
        if stage == 1:
            P.op(ACT, lambda e: e.activation(out=zb, in_=xt, func=AF.Copy), reads=xk, writes=kzb)
            P.op(ACT, lambda e: e.activation(out=zq, in_=xt, func=AF.Square), reads=xk, writes=kzq)
        elif stage == 2:
            bm = self.bank()
            bq = self.bank()
            for k in range(KC):
                P.op(PE, lambda e, k=k: e.matmul(self.ps[bm][:, :], self.ones[:], zb[:, k, :], start=(k == 0), stop=(k == KC - 1)),
                     reads=kzb + ["ones"], writes=[("ps", bm)])
            for k in range(KC):
                P.op(PE, lambda e, k=k: e.matmul(self.ps[bq][:, :], self.ones[:], zq[:, k, :], start=(k == 0), stop=(k == KC - 1)),
                     reads=kzq + ["ones"], writes=[("ps", bq)])
            P.op(ACT, lambda e: e.activation(out=mn, in_=self.ps[bm][:, :], func=AF.Copy), reads=[("ps", bm)], writes=kmn)
            P.op(DVE, lambda e: e.tensor_tensor(out=rs, in0=mn, in1=mn, op=ALU.mult), reads=kmn, writes=krs)
            P.op(DVE, lambda e: e.scalar_tensor_tensor(out=rs, in0=self.ps[bq][:, :], scalar=EPS_P, in1=rs,
                                                       op0=ALU.add, op1=ALU.subtract), reads=[("ps", bq)] + krs, writes=krs)
            P.op(ACT, lambda e: e.activation(out=rs, in_=rs, func=AF.Sqrt), reads=krs, writes=krs)
            P.op(DVE, lambda e: e.reciprocal(out=rs, in_=rs), reads=krs, writes=krs)
        elif stage == 3:
            mb = mn.unsqueeze(1).broadcast_to([128, KC, TS])
            rb = rs.unsqueeze(1).broadcast_to([128, KC, TS])
            P.op(DVE, lambda e: e.tensor_tensor(out=xt, in0=xt, in1=mb, op=ALU.subtract), reads=xk + kmn, writes=xk)
            P.op(DVE, lambda e: e.tensor_tensor(out=xt, in0=xt, in1=rb, op=ALU.mult), reads=xk + krs, writes=xk)
        elif stage == 5:
            if not last:
                for k in range(KC):
                    xs = self.x[:, k, t * TS:(t + 1) * TS]
                    P.op(ACT, lambda e, xs=xs, k=k: e.activation(
                        out=self.h[:, k, t * TS:(t + 1) * TS], in_=xs, func=AF.Identity,
                        scale=self.der[:, npar, nj, k, r:r + 1], bias=self.der[:, npar, 3 + nj, k, r:r + 1]),
                        reads=[("x", k, t), ("der", npar, nj), ("der", npar, 3 + nj)], writes=[("h", k, t)])
        else:
            for k in range(KC):
                xs = self.x[:, k, t * TS:(t + 1) * TS]
                if k % 2 == 0:
                    P.op(ACT, lambda e, xs=xs, k=k: e.activation(
                        out=xs, in_=xs, func=AF.Identity, scale=self.lnT[:, l, j, 0, k:k + 1], bias=self.lnT[:, l, j, 1, k:k + 1]),
                        reads=[("x", k, t), "lnT"], writes=[("x", k, t)])
                else:
                    P.op(DVE, lambda e, xs=xs, k=k: e.tensor_scalar(
                        out=xs, in0=xs, scalar1=self.lnT[:, l, j, 0, k:k + 1], scalar2=self.lnT[:, l, j, 1, k:k + 1],
                        op0=ALU.mult, op1=ALU.add), reads=[("x", k, t), "lnT"], writes=[("x", k, t)])

    def ln_cb(self, l, j, last=False):
        order = ((3, 2), (2, 1), (5, 2), (1, 0), (4, 3))

        nxt = self.next_sub
        tile_fn = None
        if nxt is not None and nxt[0] == "ffn" and not last:
            st = self.ffn_head_prepare(nxt[1], nxt[2])
            self.next_head_state = st
            tile_fn = lambda t: self.ffn_head_tile(st, t)
            order = tuple(o for o in order if o[0] != 4)
            self.ln_deferred = [(l, j, last, t) for t in range(NT)]
        elif nxt is not None and nxt[0] == "mix" and self.mixers and nxt[1] % 3 in (0, 2) and not last:
            order = tuple(o for o in order if o[0] != 4)
            self.ln_deferred = [(l, j, last, t) for t in range(NT)]

        def step(i):
            for (stage, lag) in order:
                t = i - lag
                if 0 <= t < NT:
                    self.ln_stage(l, j, last, stage, t)
            if tile_fn is not None and 0 <= i - 3 < NT:
                tile_fn(i - 3)

        def cb(t):
            step(t)
            if t == NT - 1:
                for i in range(NT, NT + 4):
                    self.ln_pending.append(lambda i=i: step(i))
        return cb

    def take_deferred(self):
        if not self.ln_deferred:
            return None
        dfr = {d[3]: d for d in self.ln_deferred}
        self.ln_deferred = []
        return lambda t, dfr=dfr: self.ln_stage(dfr[t][0], dfr[t][1], dfr[t][2], 4, t)

    def ln_step(self):
        if self.ln_pending:
            self.ln_pending.pop(0)()

    def ln_flush(self):
        while self.ln_pending:
            self.ln_pending.pop(0)()

    FFN_GROUPS = [[0, 2], [4, 6], [8, 10], [12, 14], [16], [18, 20]]

    def ffn_unit(self, s, gi, jj, t):
        P = self.P
        hk = [("h", k, t) for k in range(KC)]
        a = gi * 2 + jj
        bg = self.bank()
        bu = self.bank()
        for k in range(KC):
            P.op(PE, lambda e, k=k: e.matmul(
                self.ps[bg][:, :], self.w1v(s, k, jj * 128, 128), self.h[:, k, t * TS:(t + 1) * TS],
                start=(k == 0), stop=(k == KC - 1)), reads=[("w1", s)] + hk, writes=[("ps", bg)])
        for k in range(KC):
            P.op(PE, lambda e, k=k: e.matmul(
                self.ps[bu][:, :], self.w1v(s, k, 256 + jj * 128, 128), self.h[:, k, t * TS:(t + 1) * TS],
                start=(k == 0), stop=(k == KC - 1)), reads=[("w1", s)] + hk, writes=[("ps", bu)])
        sg, ksg = self.sview(5632 + self.sgi * 512, (TS,), F32)
        self.sgi = (self.sgi + 1) % 2
        P.op(ACT, lambda e: e.activation(out=sg, in_=self.ps[bg][:, :], func=AF.Silu), reads=[("ps", bg)], writes=ksg)
        P.op(DVE, lambda e: e.tensor_tensor(out=self.abuf[:, a, t * TS:(t + 1) * TS], in0=self.ps[bu][:, :], in1=sg,
                                            op=ALU.mult), reads=[("ps", bu)] + ksg, writes=[("a", a, t)])

    def ffn_head_prepare(self, l, f):
        wgu = self.d_wgu[l, f].rearrange("(k p) n -> p k n", p=128)
        slots1 = []
        for j0 in self.FFN_GROUPS[0]:
            s, _ = self.load_w1([(wgu[:, :, j0 * 128:(j0 + 2) * 128], 256),
                                 (wgu[:, :, DFF + j0 * 128:DFF + (j0 + 2) * 128], 256)])
            slots1.append(s)
        return {"l": l, "f": f, "slots1": slots1}

    def ffn_head_tile(self, st, t):
        tag = self.P.tag
        self.P.tag = "L%d.ffn%d.head" % (st["l"], st["f"])
        for gi in range(len(self.FFN_GROUPS[0])):
            for jj in range(2):
                self.ffn_unit(st["slots1"][gi], gi, jj, t)
        self.P.tag = tag

    def ffn(self, l, f, last=False, head=None):
        P = self.P
        j = 0 if f == 0 else 2
        wgu = self.d_wgu[l, f].rearrange("(k p) n -> p k n", p=128)
        wdn = self.d_wdn[l, f].rearrange("(c p) n -> p c n", p=128)
        groups = self.FFN_GROUPS
        base_tag = P.tag
        for gno, grp in enumerate(groups):
            P.tag = base_tag + ".g%d.p1" % gno
            slots2 = []
            if gno == 0:
                if head is None:
                    head = self.ffn_head_prepare(l, f)
                    for t in range(NT):
                        self.ffn_head_tile(head, t)
                        self.ln_step()
                self.ln_flush()
                for gi, j0 in enumerate(grp):
                    slots2.append((self.load_w2(wdn[:, j0:j0 + 2, :], 2, D), gi * 2, 2))
            else:
                slots1 = []
                for gi, j0 in enumerate(grp):
                    s, _ = self.load_w1([(wgu[:, :, j0 * 128:(j0 + 2) * 128], 256),
                                         (wgu[:, :, DFF + j0 * 128:DFF + (j0 + 2) * 128], 256)])
                    slots1.append(s)
                    slots2.append((self.load_w2(wdn[:, j0:j0 + 2, :], 2, D), gi * 2, 2))
                for gi in range(len(grp)):
                    for t in range(NT):
                        for jj in range(2):
                            self.ffn_unit(slots1[gi], gi, jj, t)
            P.tag = base_tag + ".g%d.mod" % gno
            is_last = (gno == len(groups) - 1)
            first_ffn = (l == 0 and f == 0)
            self.mod_step(2 if (is_last and first_ffn) else 1)
            if not first_ffn and not is_last:
                self.mod_prefetch()
            if is_last:
                if l == 0 and f == 0:
                    self.derive(0, (1, 2))
                if f == 1 and l + 1 < self.n_layers:
                    self.mod_step(18)
                    self.derive(l + 1)
            P.tag = base_tag + ".g%d.p2" % gno
            bt = self.take_deferred() if gno == 0 else None
            self.phase2(l, j, slots2, len(grp) * 2, range(KC), after_tile=self.ln_cb(l, j, last) if is_last else None,
                        before_tile=bt)
            if not is_last and first_ffn:
                P.tag = base_tag + ".g%d.mod" % gno
                self.mod_step(1)
        P.tag = base_tag

    def build(self):
        P, nc = self.P, self.nc
        self.declare()
        self.prologue()
        self.compute_mod(0)
        self.mod_step(6)
        self.derive(0, (0,))
        self.modulate_first()
        nl = self.n_layers
        subs = []
        for l in range(nl):
            subs += [("ffn", l, 0), ("mix", l), ("ffn", l, 1)]
        head = None
        for i, sub in enumerate(subs):
            self.next_sub = subs[i + 1] if i + 1 < len(subs) else None
            self.next_head_state = None
            l = sub[1]
            if sub[0] == "ffn":
                f = sub[2]
                P.tag = "L%d.ffn%d" % (l, f)
                if f == 0:
                    if l + 1 < nl:
                        self.compute_mod(l + 1)
                    self.ffn(l, 0, head=head)
                else:
                    self.ffn(l, 1, last=(l == nl - 1), head=head)
            else:
                P.tag = "L%d.mix" % l
                kind = l % 3
                if self.mixers:
                    if kind == 0:
                        self.conv_mixer(l)
                    elif kind == 1:
                        self.pool_mixer(l)
                    else:
                        self.attn_mixer(l)
                else:
                    cb = self.ln_cb(l, 1)
                    for t in range(NT):
                        cb(t)
            head = self.next_head_state
        self.ln_flush()
        assert not self.ln_deferred
        yv = self.o_yT.rearrange("(k p) t -> p k t", p=128)
        for t in range(NT):
            P.dma(SP, "yout", yv[:, :, t * TS:(t + 1) * TS], self.x[:, :, t * TS:(t + 1) * TS],
                  reads=[("x", k, t) for k in range(KC)])
        outs = ["yout"] + [k for k in ("kvout0", "kvout1") if k in P.dma_cnt]
        P.wait_all_dma(SP, outs)
        P.prepare(nc)
        with nc.Block() as block:
            P.emit(nc, block)
        return nc


    def skeys(self, off_words, nwords):
        return [("scr", c) for c in range(off_words // 128, (off_words + nwords + 127) // 128)]

    def conv_mixer(self, l):
        P = self.P
        self.ln_flush()
        ci = l // 3
        par = l % 2
        win = self.d_cwin[ci].rearrange("(k p) n -> p k n", p=128)
        wout = self.d_cwout[ci].rearrange("(c p) n -> p c n", p=128)
        ub, kub = self.sview(0, (NTOK,), F32)
        cb, kcb = self.sview(2560, (NTOK,), F32)
        vt = [self.sview(5120 + i * 512, (TS,), F32) for i in range(2)]
        vti = 0
        for grp in range(2):
            slots2 = []
            for oo in range(4):
                o = grp * 4 + oo
                s, _ = self.load_w1([(win[:, :, o * 128:(o + 1) * 128], 128),
                                     (win[:, :, D + o * 128:D + (o + 1) * 128], 128),
                                     (win[:, :, 2 * D + o * 128:2 * D + (o + 1) * 128], 128)])
                if oo % 2 == 0:
                    s2 = self.load_w2(wout[:, o:o + 2, :], 2, D)
                    slots2.append((s2, oo, 2))
                for t in range(NT):
                    hk = [("h", k, t) for k in range(KC)]
                    bc = self.bank()
                    bv = self.bank()
                    for (bb, c0) in ((bc, 128), (bv, 256)):
                        for k in range(KC):
                            P.op(PE, lambda e, s=s, k=k, t=t, bb=bb, c0=c0: e.matmul(
                                self.ps[bb][:, :], self.w1v(s, k, c0, 128), self.h[:, k, t * TS:(t + 1) * TS],
                                start=(k == 0), stop=(k == KC - 1)),
                                reads=[("w1", s)] + hk, writes=[("ps", bb)])
                    vv, kvv = vt[vti]
                    vti = (vti + 1) % 2
                    P.op(ACT, lambda e, vv=vv, bv=bv: e.activation(out=vv, in_=self.ps[bv][:, :], func=AF.Copy),
                         reads=[("ps", bv)], writes=kvv)
                    P.op(DVE, lambda e, vv=vv, bc=bc, t=t: e.tensor_tensor(
                        out=ub[:, t * TS:(t + 1) * TS], in0=self.ps[bc][:, :], in1=vv, op=ALU.mult),
                        reads=[("ps", bc)] + kvv, writes=self.skeys(t * TS, TS))
                self.mod_step(1)
                self.mod_prefetch()
                k0 = self.ckT[:, ci, 0, o:o + 1]
                k1 = self.ckT[:, ci, 1, o:o + 1]
                k2 = self.ckT[:, ci, 2, o:o + 1]
                for (s0, e0) in SEGS:
                    P.op(DVE, lambda e, s0=s0, e0=e0, k1=k1: e.tensor_scalar(
                        out=cb[:, s0:e0], in0=ub[:, s0:e0], scalar1=k1, scalar2=None, op0=ALU.mult),
                        reads=kub + ["ckT"], writes=kcb)
                    P.op(DVE, lambda e, s0=s0, e0=e0, k0=k0: e.scalar_tensor_tensor(
                        out=cb[:, s0 + 1:e0], in0=ub[:, s0:e0 - 1], scalar=k0, in1=cb[:, s0 + 1:e0],
                        op0=ALU.mult, op1=ALU.add), reads=kub + kcb + ["ckT"], writes=kcb)
                    P.op(DVE, lambda e, s0=s0, e0=e0, k2=k2: e.scalar_tensor_tensor(
                        out=cb[:, s0:e0 - 1], in0=ub[:, s0 + 1:e0], scalar=k2, in1=cb[:, s0:e0 - 1],
                        op0=ALU.mult, op1=ALU.add), reads=kub + kcb + ["ckT"], writes=kcb)
                for t in range(NT):
                    hk = [("h", k, t) for k in range(KC)]
                    bb = self.bank()
                    for k in range(KC):
                        P.op(PE, lambda e, s=s, k=k, t=t, bb=bb: e.matmul(
                            self.ps[bb][:, :], self.w1v(s, k, 0, 128), self.h[:, k, t * TS:(t + 1) * TS],
                            start=(k == 0), stop=(k == KC - 1)),
                            reads=[("w1", s)] + hk, writes=[("ps", bb)])
                    P.op(DVE, lambda e, bb=bb, t=t, oo=oo: e.tensor_tensor(
                        out=self.abuf[:, oo, t * TS:(t + 1) * TS], in0=self.ps[bb][:, :], in1=cb[:, t * TS:(t + 1) * TS],
                        op=ALU.mult), reads=[("ps", bb)] + kcb, writes=[("a", oo, t)])
            self.phase2(l, 1, slots2, 4, range(KC), after_tile=self.ln_cb(l, 1) if grp == 1 else None,
                        before_tile=self.take_deferred() if grp == 0 else None)

    def pool_mixer(self, l):
        P = self.P
        self.ln_flush()
        par = l % 2
        PADW = 8
        passes = [[(0, LS, 0)], [(LS, LP, 1), (LS + LP, LP, 1)]]
        for g in range(4):
            w = 2 << g
            s2 = self.load_w2(self.d_poolw[g].rearrange("(c p) n -> p c n", p=128), 2, 256)
            for a in range(2):
                kc = 2 * g + a
                for segs in passes:
                    N = sum(ln + 2 * PADW for (_, ln, _) in segs)
                    bufs = [self.sview(i * 2064, (N,), F32) for i in range(3)]
                    hp, khp = bufs[0]
                    off = 0
                    offs = []
                    for (t0, ln, r) in segs:
                        P.op(DVE, lambda e, off=off, hp=hp: e.memset(hp[:, off:off + PADW], 0.0), writes=khp)
                        P.op(DVE, lambda e, off=off, ln=ln, hp=hp: e.memset(hp[:, off + PADW + ln:off + 2 * PADW + ln], 0.0), writes=khp)
                        offs.append(off + PADW)
                        off += ln + 2 * PADW
                    for (t0, ln, r), o0 in zip(segs, offs):
                        P.op(ACT, lambda e, t0=t0, ln=ln, r=r, o0=o0, hp=hp, kc=kc: e.activation(
                            out=hp[:, o0:o0 + ln], in_=self.x[:, kc, t0:t0 + ln], func=AF.Identity,
                            scale=self.der[:, par, 6, kc, r:r + 1], bias=self.modT[:, par, 3, kc, r:r + 1]),
                            reads=[("x", kc, t) for t in range(t0 // TS, (t0 + ln + TS - 1) // TS)] +
                                  [("der", par, 6), ("mod", par)], writes=khp)
                    if segs is passes[1]:
                        self.mod_step(1)
                        self.mod_prefetch()
                    cur, kcur = hp, khp
                    nb = 1
                    for lev in range(1, g + 2):
                        nxt, knxt = bufs[nb]
                        if lev == 1:
                            P.op(DVE, lambda e, cur=cur, nxt=nxt, N=N: e.tensor_tensor(
                                out=nxt[:, 1:N], in0=cur[:, 0:N - 1], in1=cur[:, 1:N], op=ALU.add),
                                reads=kcur, writes=knxt)
                        else:
                            d = 1 << (lev - 2)
                            P.op(DVE, lambda e, cur=cur, nxt=nxt, N=N, d=d: e.tensor_tensor(
                                out=nxt[:, 2 * d:N - 2 * d], in0=cur[:, d:N - 3 * d], in1=cur[:, 3 * d:N - d], op=ALU.add),
                                reads=kcur, writes=knxt)
                        cur, kcur = nxt, knxt
                        nb = 2 if nb == 1 else 1
                    for (t0, ln, r), o0 in zip(segs, offs):
                        P.op(DVE, lambda e, cur=cur, t0=t0, ln=ln, o0=o0, hp=hp, a=a, w=w: e.scalar_tensor_tensor(
                            out=self.abuf[:, a, t0:t0 + ln], in0=cur[:, o0:o0 + ln], scalar=1.0 / w, in1=hp[:, o0:o0 + ln],
                            op0=ALU.mult, op1=ALU.subtract),
                            reads=kcur + khp, writes=[("a", a, t) for t in range(t0 // TS, (t0 + ln + TS - 1) // TS)])
                        tmp, ktmp = self.sview(6192 + 0, (8,), F32)
                        for side in range(2):
                            e0 = o0 if side == 0 else o0 + ln - 8
                            a0 = t0 if side == 0 else t0 + ln - 8
                            P.op(DVE, lambda e, cur=cur, e0=e0, side=side, tmp=tmp, g=g: e.tensor_tensor(
                                out=tmp, in0=cur[:, e0:e0 + 8], in1=self.iedge[:, g, side, :], op=ALU.mult),
                                reads=kcur + ["iedge"], writes=ktmp)
                            P.op(DVE, lambda e, e0=e0, a0=a0, tmp=tmp, hp=hp, a=a: e.tensor_tensor(
                                out=self.abuf[:, a, a0:a0 + 8], in0=tmp, in1=hp[:, e0:e0 + 8], op=ALU.subtract),
                                reads=ktmp + khp, writes=[("a", a, a0 // TS)])
            self.mod_step(1)
            self.mod_prefetch()
            self.phase2(l, 1, [(s2, 0, 2)], 2, [2 * g, 2 * g + 1], ncols=256, col_of=lambda o, g=g: (o - 2 * g) * 128,
                        after_tile=self.ln_cb(l, 1) if g == 3 else None)

    def attn_mixer(self, l):
        P = self.P
        self.ln_flush()
        par = l % 2
        KT_O, VA_O, PT_O, RP_O = 0, 1536, 3840, 4608
        kT = self.scr[:, KT_O:KT_O + 1536].bitcast(BF16)
        va = self.scr[:, VA_O:VA_O + 2304].bitcast(BF16).rearrange("p (t c) -> p t c", t=24)
        kkT = lambda i0, n=1: self.skeys(KT_O + i0 * 64, n * 64)
        kva = lambda i0, n=1: self.skeys(VA_O + i0 * 96, n * 96)
        pts = [(self.scr[:, PT_O + i * 256:PT_O + (i + 1) * 256].bitcast(BF16), self.skeys(PT_O + i * 256, 256))
               for i in range(3)]
        cs = self.scr[:, RP_O:RP_O + 1024].rearrange("p (a n) -> p a n", a=2)
        kcs = self.skeys(RP_O, 1024)
        t1, kt1 = self.sview(RP_O + 1024, (TS,), F32)
        t2, kt2 = self.sview(RP_O + 1536, (TS,), F32)
        rec, krec = self.sview(RP_O, (TS,), F32)
        rec2, krec2 = self.sview(RP_O + 512, (TS,), F32)
        eh = self.scr[0:1, RP_O + 1024:RP_O + 1280].bitcast(BF16)
        keh = self.skeys(RP_O + 1024, 256)
        el = self.scr[0:1, RP_O + 1280:RP_O + 1536].bitcast(BF16)
        kel = self.skeys(RP_O + 1280, 256)
        stg = [self.sview(RP_O + 1024 + i * 512, (TS,), F32) for i in range(2)]
        qb_ = lambda cc: self.abuf[:, 2 + cc, :]

        P.dma(POOL, "trim", self.trim[:], self.d_tri, writes=["trim"])
        P.dma(POOL, "ident", self.ident[:], self.d_ident, writes=["ident"])
        P.dma(SP, "sink", self.es[:], self.d_sink, writes=["es"])
        P.op(ACT, lambda e: e.activation(out=self.es[:], in_=self.es[:], func=AF.Exp), reads=["es"], writes=["es"])
        P.op(DVE, lambda e: e.memset(self.ones1[:], 1.0), writes=["ones1"])
        P.op(DVE, lambda e: e.memset(va, 1.0), writes=kva(0, 24))

        s, _ = self.load_w1([(self.d_wkv.rearrange("(k p) n -> p k n", p=128), 512)])
        for blk in range(4):
            tok0 = LS + blk * 128
            b = self.bank()
            for k in range(KC):
                P.op(PE, lambda e, s=s, k=k, b=b, tok0=tok0: e.matmul(
                    self.ps[b][:, :], self.h[:, k, tok0:tok0 + 128], self.w1v(s, k, 0, 512),
                    start=(k == 0), stop=(k == KC - 1)),
                    reads=[("w1", s)] + [("h", k, 4) for k in range(KC)], writes=[("ps", b)])
            sg, ksg = stg[blk % 2]
            P.op(ACT, lambda e, sg=sg, b=b: e.activation(out=sg, in_=self.ps[b][:, :], func=AF.Copy),
                 reads=[("ps", b)], writes=ksg)
            P.dma(SP, "kvout%d" % (blk % 2), self.o_kv[blk * 128:(blk + 1) * 128, :], sg, reads=ksg)

        acc_i = 0
        sb_i = 0
        pt_i = 0
        for j in range(4):
            wq = self.d_wqk[j].rearrange("(k p) n -> p k n", p=128)
            sA, _ = self.load_w1([(wq[:, :, 0:512], 512)])
            sB, _ = self.load_w1([(wq[:, :, 512:832], 320)])
            s2 = self.load_w2(self.d_wo.rearrange("(c p) n -> p c n", p=128)[:, 2 * j:2 * j + 2, :], 2, D)
            P.dma(POOL, "kctx", kT[:, 2560:3072], self.d_kctx[j], writes=kkT(20, 4))
            P.dma(POOL, "vctx", va[:, 20:24, 64:128], self.d_vctx[:, j], writes=kva(20, 4))
            for t in range(NT):
                hk = [("h", k, t) for k in range(KC)]
                tsl = slice(t * TS, (t + 1) * TS)
                rope = t < 4
                if rope:
                    P.dma(SP, "cs", cs[:, 0, :], self.d_cos[:, tsl], writes=kcs)
                    P.dma(SP, "cs", cs[:, 1, :], self.d_sin[:, tsl], writes=kcs)
                jobs = [(sA, 0, 256, qb_(0)[:, tsl], [("a", 2, t)]), (sA, 128, 384, qb_(1)[:, tsl], [("a", 3, t)]),
                        (sB, 0, 128, kT[:, tsl], kkT(4 * t, 4))]
                for (sw, c_main, c_swap, dst, kdst) in jobs:
                    b1 = self.bank()
                    for k in range(KC):
                        P.op(PE, lambda e, sw=sw, k=k, b1=b1, c_main=c_main, tsl=tsl: e.matmul(
                            self.ps[b1][:, :], self.w1v(sw, k, c_main, 128), self.h[:, k, tsl],
                            start=(k == 0), stop=(k == KC - 1)), reads=[("w1", sw)] + hk, writes=[("ps", b1)])
                    if rope:
                        b2 = self.bank()
                        for k in range(KC):
                            P.op(PE, lambda e, sw=sw, k=k, b2=b2, c_swap=c_swap, tsl=tsl: e.matmul(
                                self.ps[b2][:, :], self.w1v(sw, k, c_swap, 128), self.h[:, k, tsl],
                                start=(k == 0), stop=(k == KC - 1)), reads=[("w1", sw)] + hk, writes=[("ps", b2)])
                        P.op(DVE, lambda e, b1=b1: e.tensor_tensor(out=t1, in0=self.ps[b1][:, :], in1=cs[:, 0, :], op=ALU.mult),
                             reads=[("ps", b1)] + kcs, writes=kt1)
                        P.op(DVE, lambda e, b2=b2: e.tensor_tensor(out=t2, in0=self.ps[b2][:, :], in1=cs[:, 1, :], op=ALU.mult),
                             reads=[("ps", b2)] + kcs, writes=kt2)
                        P.op(DVE, lambda e, dst=dst: e.tensor_tensor(out=dst, in0=t1, in1=t2, op=ALU.add),
                             reads=kt1 + kt2, writes=kdst)
                    else:
                        P.op(ACT, lambda e, dst=dst, b1=b1: e.activation(out=dst, in_=self.ps[b1][:, :], func=AF.Copy),
                             reads=[("ps", b1)], writes=kdst)
                bv = self.bank()
                for blk in range(4):
                    tok0 = t * TS + blk * 128
                    for k in range(KC):
                        P.op(PE, lambda e, k=k, bv=bv, blk=blk, tok0=tok0, sB=sB: e.matmul(
                            self.ps[bv][:, blk * 64:(blk + 1) * 64], self.h[:, k, tok0:tok0 + 128], self.w1v(sB, k, 256, 64),
                            start=(k == 0), stop=(k == KC - 1)), reads=[("w1", sB)] + hk, writes=[("ps", bv)])
                P.op(ACT, lambda e, bv=bv, t=t: e.activation(
                    out=va[:, 4 * t:4 * t + 4, 64:128], in_=self.ps[bv][:, 0:256].rearrange("p (a d) -> p a d", a=4),
                    func=AF.Copy), reads=[("ps", bv)], writes=kva(4 * t, 4))
            self.mod_step(1)
            self.mod_prefetch()
            esb = self.es[0:1, 4 * j:4 * j + 4].unsqueeze(2).broadcast_to([1, 4, 128])
            eh3 = eh.rearrange("p (a q) -> p a q", a=4)
            el3 = el.rearrange("p (a q) -> p a q", a=4)
            P.op(DVE, lambda e, esb=esb: e.tensor_copy(out=eh3, in_=esb), reads=["es"], writes=keh)
            P.op(DVE, lambda e, esb=esb: e.tensor_tensor(out=el3, in0=esb, in1=eh3, op=ALU.subtract),
                 reads=["es"] + keh, writes=kel)
            blocks = []
            for qb in range(16):
                kts = [(kb, (0 if kb == qb - 1 else (1 if kb == qb + 1 else None)))
                       for kb in (qb - 1, qb, qb + 1) if 0 <= kb < 16]
                kts += [(20 + i, None) for i in range(4)]
                blocks.append((qb * 128, kts))
            for sq in range(2):
                for pb in range(2):
                    blocks.append((LS + sq * LP + pb * 128, [(16 + sq * 2 + i, None) for i in range(2)]))
            units = []
            for bi, (q0, kts) in enumerate(blocks):
                for ki, (kt_, msk) in enumerate(kts):
                    units.append((bi, q0, ki, len(kts), kt_, msk))
            LAG = 2
            state = {}

            def front(u, ui):
                (bi, q0, ki, nk, kt_, msk) = u
                tq = q0 // TS
                qk = [("a", 2, tq), ("a", 3, tq)]
                bs2 = (4, 5) if ui % 2 == 0 else (6, 7)
                pt, kpt = pts[ui % 3]
                kcols = slice(kt_ * 128, (kt_ + 1) * 128)
                for half in range(2):
                    p0 = half * 64
                    bs_ = bs2[half]
                    P.op(PE, lambda e, bs_=bs_, p0=p0, kcols=kcols, q0=q0, msk=msk: e.matmul(
                        self.ps[bs_][:, 0:256], kT[p0:p0 + 64, kcols],
                        self.abuf[p0:p0 + 64, 2:4, q0:q0 + 128], start=True, stop=(msk is None)),
                        reads=kkT(kt_) + qk, writes=[("ps", bs_)])
                if msk is not None:
                    for half in range(2):
                        bs_ = bs2[half]
                        P.op(PE, lambda e, bs_=bs_, msk=msk: e.matmul(
                            self.ps[bs_][:, 0:256], self.ident[:], self.trim[:, msk, :], start=False, stop=True),
                            reads=["ident", "trim"], writes=[("ps", bs_)])
                for half in range(2):
                    bs_ = bs2[half]
                    P.op(ACT, lambda e, pt=pt, bs_=bs_, half=half: e.activation(
                        out=pt[:, half * 256:(half + 1) * 256], in_=self.ps[bs_][:, 0:256], func=AF.Exp, scale=ATT_SCALE),
                        reads=[("ps", bs_)], writes=kpt)

            def back(u, ui):
                (bi, q0, ki, nk, kt_, msk) = u
                tq = q0 // TS
                bn, bd = (0, 1) if bi % 2 == 0 else (2, 3)
                pt, kpt = pts[ui % 3]
                first = (ki == 0)
                last = (ki == nk - 1)
                P.op(PE, lambda e: e.matmul(self.ps[bn][:, 0:256], va[:, kt_, 64:192], pt[:, 0:256],
                                            start=first, stop=last, skip_group_check=True),
                     reads=kva(kt_) + kpt, writes=[("ps", bn)])
                P.op(PE, lambda e: e.matmul(self.ps[bn][:, 256:512], va[:, kt_, 0:128], pt[:, 256:512],
                                            start=False, stop=last, skip_group_check=True),
                     reads=kva(kt_) + kpt, writes=[("ps", bn)])
                P.op(PE, lambda e: e.matmul(self.ps[bd][:, :], self.ones1[:], pt[:, :], start=first, stop=False),
                     reads=["ones1"] + kpt, writes=[("ps", bd)])
                if not last:
                    return
                P.op(PE, lambda e: e.matmul(self.ps[bd][:, :], self.ones1[0:1, :], eh[0:1, :], start=False, stop=False),
                     reads=["ones1"] + keh, writes=[("ps", bd)])
                P.op(PE, lambda e: e.matmul(self.ps[bd][:, :], self.ones1[0:1, :], el[0:1, :], start=False, stop=True),
                     reads=["ones1"] + kel, writes=[("ps", bd)])
                P.op(DVE, lambda e: e.reciprocal(out=rec, in_=self.ps[bd][:, :]), reads=[("ps", bd)], writes=krec)
                for (nlo, c0) in ((0, 0), (64, 256)):
                    P.op(DVE, lambda e, nlo=nlo, c0=c0: e.tensor_tensor(
                        out=self.abuf[nlo:nlo + 64, 0:2, q0:q0 + 128],
                        in0=self.ps[bn][nlo:nlo + 64, c0:c0 + 256].rearrange("p (a q) -> p a q", a=2),
                        in1=rec[nlo:nlo + 64, c0:c0 + 256].rearrange("p (a q) -> p a q", a=2), op=ALU.mult),
                        reads=[("ps", bn)] + krec, writes=[("a", 0, tq), ("a", 1, tq)])

            for ui in range(len(units) + LAG):
                if ui < len(units):
                    front(units[ui], ui)
                if ui >= LAG:
                    back(units[ui - LAG], ui - LAG)
            self.mod_step(1)
            self.mod_prefetch()
            self.phase2(l, 1, [(s2, 0, 2)], 2, range(KC), after_tile=self.ln_cb(l, 1) if j == 3 else None,
                        before_tile=self.take_deferred() if j == 0 else None)


def _host_inputs(x_prompt, x_sample, cache_ctx_k, cache_ctx_v, c, c_ctx, w_mod, b_mod, ln_g, ln_b,
                 ffn_w_gate_up, ffn_w_down, conv_w_in, conv_k, conv_w_out, pool_w, pool_scale,
                 attn_w_qkv, attn_w_o, attn_sink, cores=range(NCORES)):
    f = lambda a: np.ascontiguousarray(np.asarray(a, np.float32))
    x_prompt, x_sample = f(x_prompt), f(x_sample)
    cache_ctx_k, cache_ctx_v = f(cache_ctx_k), f(cache_ctx_v)
    c, c_ctx = f(c), f(c_ctx)
    wq = f(attn_w_qkv)[0]
    q_w, k_w, v_w = wq[:, :1024], wq[:, 1024:1280], wq[:, 1280:1536]
    wqk = np.zeros((4, D, 832), np.float32)
    for j in range(4):
        qc = q_w[:, j * 256:(j + 1) * 256]
        kj = k_w[:, j * 64:(j + 1) * 64]
        kd = np.concatenate([kj, kj], axis=1)
        wqk[j] = np.concatenate([qc, _swap32(qc), kd, _swap32(kd), v_w[:, j * 64:(j + 1) * 64]], axis=1)
    cosT, sinT = _rope_tables()
    tri = np.zeros((128, 2, 2, 128), np.float32)
    rr, cc = np.arange(128)[:, None], np.arange(128)[None, :]
    tri[:, 0, :, :] = np.where(rr >= cc, 0.0, -30000.0)[:, None, :]
    tri[:, 1, :, :] = np.where(rr <= cc, 0.0, -30000.0)[:, None, :]
    tri = tri.reshape(128, 2, 256)
    sink = f(attn_sink)[0]
    sinkp = np.zeros((4, 2, 2), np.float32)
    for j in range(4):
        for par in range(2):
            for cc_ in range(2):
                sinkp[j, par, cc_] = sink[4 * j + 2 * cc_ + par]
    shared = {
        "w_mod": f(w_mod), "bmodT": np.ascontiguousarray(np.moveaxis(f(b_mod).reshape(DEPTH, 72, 128), -1, 0)),
        "lnT": np.ascontiguousarray(np.stack([_fm(ln_g), _fm(ln_b)], axis=3)),
        "w_gu": f(ffn_w_gate_up), "w_dn": f(ffn_w_down),
        "conv_w_in": f(conv_w_in), "convkT": _fm(conv_k), "conv_w_out": f(conv_w_out),
        "pool_w": f(pool_w)[0], "poolsT": _fm(f(pool_scale)[0]), "iedge": _pool_inv_counts(),
        "wqk": wqk, "wkv": np.ascontiguousarray(wq[:, 1024:1536]), "w_o": f(attn_w_o)[0],
        "sinkp": sinkp.reshape(1, 16), "cosT": cosT, "sinT": sinT, "trim": tri, "ident": np.eye(128, dtype=np.float32),
    }
    in_maps = []
    for b in cores:
        xT = np.concatenate([x_sample[b].T, x_prompt[2 * b].T, x_prompt[2 * b + 1].T], axis=1)
        cT = np.stack([_fm(c[b]), _fm(c_ctx)], axis=-1)
        ck = cache_ctx_k[b, 0]
        kT = np.transpose(ck, (1, 2, 0))
        kctxT = np.concatenate([kT, kT], axis=1)
        cv = cache_ctx_v[b, 0].reshape(4, 128, 4, HD)
        vctx = np.transpose(cv, (1, 2, 0, 3))
        m = dict(shared)
        m.update({"xT": np.ascontiguousarray(xT), "cT": np.ascontiguousarray(cT),
                  "kctxT": np.ascontiguousarray(kctxT), "vctx": np.ascontiguousarray(vctx)})
        in_maps.append(m)
    return in_maps


_NC_CACHE = {}


def _get_nc(n_layers=DEPTH, mixers=True):
    key = (n_layers, mixers)
    if key not in _NC_CACHE:
        _NC_CACHE[key] = Builder(n_layers, mixers).build()
    return _NC_CACHE[key]


def kernel(**inputs):
    in_maps = _host_inputs(**inputs)
    nc = _get_nc()
    res = run_bass_kernel_spmd(nc, in_maps, core_ids=list(range(NCORES)))
    B, S = 16, LP
    y_p = np.zeros((B, S, D), np.float32)
    y_s = np.zeros((NCORES, LS, D), np.float32)
    ctx_k = np.zeros((B, 1, S, 4, HD), np.float32)
    ctx_v = np.zeros((B, 1, S, 4, HD), np.float32)
    for b in range(NCORES):
        r = res.results[b]
        yT = np.asarray(r["yT"])
        y_s[b] = yT[:, :LS].T
        y_p[2 * b] = yT[:, LS:LS + LP].T
        y_p[2 * b + 1] = yT[:, LS + LP:].T
        kv = np.asarray(r["kvout"])
        for s in range(2):
            ctx_k[2 * b + s, 0] = kv[s * LP:(s + 1) * LP, 0:256].reshape(LP, 4, HD)
            ctx_v[2 * b + s, 0] = kv[s * LP:(s + 1) * LP, 256:512].reshape(LP, 4, HD)
    return (y_p, y_s, ctx_k, ctx_v)
```

```python
import numpy as np
import concourse.bass as bass
import concourse.mybir as mybir
from concourse.bass_utils import run_bass_kernel_spmd

F32 = mybir.dt.float32
BF16 = mybir.dt.bfloat16
AF = mybir.ActivationFunctionType
ALU = mybir.AluOpType

D = 1024
KC = 8
DEPTH = 4
DFF = 2816
NJ = DFF // 128
NTOK = 2560
TS = 512
NT = NTOK // TS
LS = 2048
LP = 256
NHEAD = 16
HD = 64
PAST = 512
ALPHA = (2.0 * DEPTH) ** 0.25
LN_EPS = 1e-5
EPS_P = LN_EPS / (ALPHA * ALPHA)
ATT_SCALE = HD ** -0.5
SEGS = ((0, 2048), (2048, 2304), (2304, 2560))
NCORES = 8

PE, ACT, DVE, POOL, SP = "pe", "act", "dve", "pool", "sp"
ENGS = (PE, ACT, DVE, POOL, SP)


class _Op:
    __slots__ = ("fn", "deps", "sig", "cnt", "dma", "tag")

    def __init__(self, fn, deps, dma=None):
        self.tag = None
        self.fn = fn
        self.deps = deps
        self.sig = False
        self.cnt = 0
        self.dma = dma


class Prog:
    def __init__(self):
        self.ops = {e: [] for e in ENGS}
        self.res = {}
        self.dma_cnt = {}
        self.tag = ""

    def _collect(self, reads, writes, me, idx):
        deps = {}
        res = self.res
        for k in reads:
            st = res.get(k)
            if st is None:
                st = res[k] = [{}, {}]
            for p, i in st[0].items():
                if deps.get(p, -1) < i:
                    deps[p] = i
        for k in writes:
            st = res.get(k)
            if st is None:
                st = res[k] = [{}, {}]
            for p, i in st[0].items():
                if deps.get(p, -1) < i:
                    deps[p] = i
            for p, i in st[1].items():
                if deps.get(p, -1) < i:
                    deps[p] = i
        for k in reads:
            st = res[k]
            if st[1].get(me, -1) < idx:
                st[1][me] = idx
        for k in writes:
            res[k] = [{me: idx}, {}]
        return deps

    def op(self, eng, fn, reads=(), writes=()):
        idx = len(self.ops[eng])
        deps = self._collect(reads, writes, eng, idx)
        o = _Op(fn, deps)
        o.tag = self.tag
        self.ops[eng].append(o)

    def dma(self, queue, slot, out, in_, reads=(), writes=()):
        n = self.dma_cnt.get(slot, 0) + 1
        self.dma_cnt[slot] = n
        me = ("dma", slot)
        deps = self._collect(reads, writes, me, n)
        deps.pop(me, None)
        self.ops[queue].append(_Op(lambda e, o=out, i=in_: e.dma_start(out=o, in_=i), deps, dma=(slot, n)))

    def wait_all_dma(self, queue, slots):
        deps = {("dma", s): self.dma_cnt[s] for s in slots if self.dma_cnt.get(s)}
        self.ops[queue].append(_Op(None, deps))

    def prepare(self, nc):
        for e in ENGS:
            for o in self.ops[e]:
                for p, i in o.deps.items():
                    if isinstance(p, str):
                        if p == e and e == PE:
                            continue
                        self.ops[p][i].sig = True
        for e in ENGS:
            c = 0
            for o in self.ops[e]:
                if o.sig:
                    c += 1
                o.cnt = c
        sems = {}
        import contextlib
        stack = contextlib.ExitStack()
        for e in ENGS:
            sems[e] = stack.enter_context(nc.semaphore("sem_" + e))
        for s in self.dma_cnt:
            sems[("dma", s)] = stack.enter_context(nc.semaphore("dsem_" + s))
        self._stack = stack
        self._sems = sems

    def emit(self, nc, block):
        sems = self._sems

        def run(e, eng):
            seen = {}
            for o in self.ops[e]:
                for p, i in o.deps.items():
                    if isinstance(p, str):
                        if p == e and e == PE:
                            continue
                        need = self.ops[p][i].cnt
                    else:
                        need = 16 * i
                    if seen.get(p, 0) < need:
                        eng.wait_ge(sems[p], need)
                        seen[p] = need
                if o.fn is None:
                    continue
                ins = o.fn(eng)
                if o.dma is not None:
                    ins.then_inc(sems[("dma", o.dma[0])], 16)
                elif o.sig:
                    ins.then_inc(sems[e], 1)

        block.tensor(lambda eng: run(PE, eng))
        block.scalar(lambda eng: run(ACT, eng))
        block.vector(lambda eng: run(DVE, eng))
        block.gpsimd(lambda eng: run(POOL, eng))
        block.sync(lambda eng: run(SP, eng))


def _fm(v):
    v = np.asarray(v, np.float32)
    lead = v.shape[:-1]
    r = v.reshape(lead + (KC, 128))
    r = np.moveaxis(r, -1, 0)
    return np.ascontiguousarray(r)


def _swap32(w):
    s = w.shape
    r = w.reshape(s[:-1] + (s[-1] // 64, 2, 32))
    return r[..., ::-1, :].reshape(s)


def _rope_tables():
    n_freq = HD // 4
    inv = (10000.0 ** (-(np.arange(n_freq, dtype=np.float32) / np.float32(n_freq)))).astype(np.float32)
    t = np.arange(LS)
    row = (t // 64).astype(np.float32)
    col = (t % 64).astype(np.float32)
    ang = np.concatenate([row[:, None] * inv, col[:, None] * inv], axis=-1).astype(np.float32)
    cos = np.cos(ang).astype(np.float32).T
    sin = np.sin(ang).astype(np.float32).T
    p = np.arange(128)
    cosT = cos[p % 32]
    sgn = np.where((p % 64) < 32, -1.0, 1.0).astype(np.float32)[:, None]
    sinT = sin[p % 32] * sgn
    return np.ascontiguousarray(cosT), np.ascontiguousarray(sinT)


def _pool_inv_counts():
    out = np.zeros((128, 4, 2, 8), np.float32)
    L = 256
    for g, w in enumerate((2, 4, 8, 16)):
        t = np.arange(L)
        lo = np.clip(t - w // 2, 0, L)
        hi = np.clip(t + w // 2, 0, L)
        inv = (1.0 / (hi - lo).astype(np.float32)).astype(np.float32)
        out[:, g, 0, :] = inv[:8]
        out[:, g, 1, :] = inv[L - 8:]
    return out


class Builder:
    def __init__(self, n_layers=DEPTH, mixers=True):
        self.n_layers = n_layers
        self.mixers = mixers
        self.nc = bass.Bass("TRN2", target_bir_lowering=False)
        self.P = Prog()
        self.psum_i = 0
        self.w1_i = 0
        self.w2_i = 0
        self.tmp_i = {}
        self.ln_pending = []
        self.ln_deferred = []
        self.next_sub = None
        self.next_head_state = None
        self.sgi = 0

    def declare(self):
        nc = self.nc
        di = lambda name, shape: nc.dram_tensor(name, list(shape), F32, kind="ExternalInput").ap()
        do = lambda name, shape: nc.dram_tensor(name, list(shape), F32, kind="ExternalOutput").ap()
        self.d_xT = di("xT", (D, NTOK))
        self.d_cT = di("cT", (128, KC, 2))
        self.d_wmod = di("w_mod", (DEPTH, D, 9 * D))
        self.d_bmodT = di("bmodT", (128, DEPTH, 72))
        self.d_lnT = di("lnT", (128, DEPTH, 3, 2, KC))
        self.d_wgu = di("w_gu", (DEPTH, 2, D, 2 * DFF))
        self.d_wdn = di("w_dn", (DEPTH, 2, DFF, D))
        self.d_cwin = di("conv_w_in", (2, D, 3 * D))
        self.d_ckT = di("convkT", (128, 2, 3, KC))
        self.d_cwout = di("conv_w_out", (2, D, D))
        self.d_poolw = di("pool_w", (4, 256, 256))
        self.d_poolsT = di("poolsT", (128, KC))
        self.d_iedge = di("iedge", (128, 4, 2, 8))
        self.d_wqk = di("wqk", (4, D, 832))
        self.d_wkv = di("wkv", (D, 512))
        self.d_wo = di("w_o", (D, D))
        self.d_sink = di("sinkp", (1, 16))
        self.d_cos = di("cosT", (128, LS))
        self.d_sin = di("sinT", (128, LS))
        self.d_tri = di("trim", (128, 2, 256))
        self.d_ident = di("ident", (128, 128))
        self.d_kctx = di("kctxT", (4, 128, PAST))
        self.d_vctx = di("vctx", (128, 4, 4, HD))
        self.o_yT = do("yT", (D, NTOK))
        self.o_kv = do("kvout", (2 * LP, 512))

        sb = lambda name, shape, dt: nc.alloc_sbuf_tensor(name, list(shape), dt)
        self.x = sb("x", (128, KC, NTOK), F32)
        self.h = sb("h", (128, KC, NTOK), BF16)
        self.abuf = sb("abuf", (128, 4, NTOK), BF16)
        self.w1 = sb("w1", (128, 3, KC * 512), BF16)
        self.w2 = sb("w2", (128, 2, 2 * D), BF16)
        self.SCRW = 6656
        self.scr = sb("scr", (128, self.SCRW), F32)
        self.cT = sb("cTs", (128, KC, 2), F32)
        self.cTb = sb("cTb", (128, KC, 2), BF16)
        self.bmodT = sb("bmodTs", (128, DEPTH, 72), F32)
        self.lnT = sb("lnTs", (128, DEPTH, 3, 2, KC), F32)
        self.ckT = sb("ckTs", (128, 2, 3, KC), F32)
        self.poolsT = sb("poolsTs", (128, KC), F32)
        self.modT = sb("modT", (128, 2, 9, KC, 2), F32)
        self.der = sb("der", (128, 2, 8, KC, 2), F32)
        self.ones = sb("ones", (128, 128), BF16)
        self.iedge = sb("iedges", (128, 4, 2, 8), F32)
        self.trim = sb("trims", (128, 2, 256), BF16)
        self.ident = sb("idents", (128, 128), BF16)
        self.ones1 = sb("ones1", (128, 128), BF16)
        self.es = sb("es", (1, 16), F32)
        self.ps = [nc.alloc_psum_tensor("ps%d" % i, [128, 512], F32) for i in range(8)]

    def bank(self):
        b = self.psum_i
        self.psum_i = (b + 1) % 8
        return b

    def sview(self, off_words, shape, dt):
        n = int(np.prod(shape))
        words = n if dt == F32 else (n + 1) // 2
        assert off_words + words <= self.SCRW, (off_words, words)
        ap = self.scr[:, off_words:off_words + words]
        if dt != F32:
            ap = ap.bitcast(dt)
        if len(shape) == 2:
            ap = ap.rearrange("p (a b) -> p a b", a=shape[0])
        elif len(shape) == 3:
            ap = ap.rearrange("p (a b c) -> p a b c", a=shape[0], b=shape[1])
        keys = [("scr", c) for c in range(off_words // 128, (off_words + words + 127) // 128)]
        return ap, keys

    def load_w1(self, parts):
        s = self.w1_i
        self.w1_i = (s + 1) % 3
        off = 0
        offs = []
        for ap, ncols in parts:
            dst = self.w1[:, s, :].rearrange("p (k n) -> p k n", k=KC)[:, :, off:off + ncols]
            self.P.dma(POOL, "w1_%d" % s, dst, ap, writes=[("w1", s)])
            offs.append(off)
            off += ncols
        assert off <= 512
        return s, offs

    def w1v(self, s, k, c0, n):
        return self.w1[:, s, :].rearrange("p (k n) -> p k n", k=KC)[:, k, c0:c0 + n]

    def load_w2(self, ap, nrow_chunks, ncols):
        s = self.w2_i
        self.w2_i = (s + 1) % 2
        dst = self.w2[:, s, 0:nrow_chunks * ncols].rearrange("p (c n) -> p c n", c=nrow_chunks)
        self.P.dma(POOL, "w2_%d" % s, dst, ap, writes=[("w2", s)])
        return s

    def w2v(self, s, c, ncols, c0, n):
        return self.w2[:, s, c * ncols + c0: c * ncols + c0 + n]

    def prologue(self):
        P, nc = self.P, self.nc
        small = [(self.cT, self.d_cT, "cT"), (self.bmodT, self.d_bmodT, "bmodT"), (self.lnT, self.d_lnT, "lnT"),
                 (self.ckT, self.d_ckT, "ckT"), (self.poolsT, self.d_poolsT, "poolsT"),
                 (self.iedge, self.d_iedge, "iedge")]
        for t, d, name in small:
            P.dma(SP, "sm_" + name, t[:], d, writes=[name])
        P.op(DVE, lambda e: e.memset(self.ones[:], 1.0 / D), writes=["ones"])
        P.op(ACT, lambda e: e.activation(out=self.cTb[:], in_=self.cT[:], func=AF.Silu), reads=["cT"], writes=["cTb"])
        xv = self.d_xT.rearrange("(k p) t -> p k t", p=128)
        for t in range(NT):
            P.dma(SP, "xin%d" % t, self.x[:, :, t * TS:(t + 1) * TS], xv[:, :, t * TS:(t + 1) * TS],
                  writes=[("x", k, t) for k in range(KC)])

    def compute_mod(self, l):
        self.mod_pending = list(getattr(self, "mod_pending", [])) + [(l, pc) for pc in range(18)]

    def mod_prefetch(self):
        if getattr(self, "mod_pref", None) is not None or not getattr(self, "mod_pending", None):
            return
        l, pc = self.mod_pending.pop(0)
        wv = self.d_wmod[l].rearrange("(k p) n -> p k n", p=128)
        s, _ = self.load_w1([(wv[:, :, pc * 512:(pc + 1) * 512], 512)])
        self.mod_pref = (l, pc, s)

    def mod_step(self, n=1):
        P = self.P
        for _ in range(n):
            if getattr(self, "mod_pref", None) is None:
                self.mod_prefetch()
            if getattr(self, "mod_pref", None) is None:
                return
            l, pc, s = self.mod_pref
            self.mod_pref = None
            par = l % 2
            b = self.bank()
            pst = self.ps[b]
            for oc in range(4):
                for k in range(KC):
                    P.op(PE, lambda e, s=s, k=k, oc=oc, pst=pst: e.matmul(
                        pst[:, oc * 2:oc * 2 + 2], self.w1v(s, k, oc * 128, 128), self.cTb[:, k, :],
                        start=(k == 0), stop=(k == KC - 1)),
                        reads=[("w1", s), "cTb"], writes=[("ps", b)])
            dst = self.modT[:, par].rearrange("p m k r -> p (m k) r")[:, pc * 4:(pc + 1) * 4, :]
            src = pst[:, 0:8].rearrange("p (c r) -> p c r", r=2)
            bm = self.bmodT[:, l, pc * 4:(pc + 1) * 4].unsqueeze(2).broadcast_to([128, 4, 2])
            P.op(DVE, lambda e, dst=dst, src=src, bm=bm: e.tensor_tensor(out=dst, in0=src, in1=bm, op=ALU.add),
                 reads=[("ps", b), "bmodT"], writes=[("mod", par)] + [("gq", par, jx) for jx in range(3)])

    def derive(self, l, js=(0, 1, 2)):
        P = self.P
        par = l % 2
        mod = self.modT[:, par]
        der = self.der[:, par]
        rd = [("mod", par), "lnT"]
        for j in js:
            sc = mod[:, 3 * j + 1]
            sh = mod[:, 3 * j]
            gt = mod[:, 3 * j + 2]
            coef = (0.5 if j != 1 else 1.0) / ALPHA
            if j == 1:
                P.op(DVE, lambda e, sc=sc: e.tensor_scalar(out=der[:, 6], in0=sc, scalar1=1.0, scalar2=None, op0=ALU.add),
                     reads=rd, writes=[("der", par, 6)])
            if l == 0 and j == 0:
                P.op(DVE, lambda e, sc=sc: e.tensor_scalar(out=der[:, 0], in0=sc, scalar1=1.0, scalar2=None, op0=ALU.add),
                     reads=rd, writes=[("der", par, 0)])
                P.op(DVE, lambda e, sh=sh: e.tensor_copy(out=der[:, 3], in_=sh), reads=rd, writes=[("der", par, 3)])
            else:
                pl, pj = (l, j - 1) if j > 0 else (l - 1, 2)
                gam = self.lnT[:, pl, pj, 0, :].unsqueeze(2).broadcast_to([128, KC, 2])
                bet = self.lnT[:, pl, pj, 1, :].unsqueeze(2).broadcast_to([128, KC, 2])
                P.op(DVE, lambda e, sc=sc, gam=gam, j=j: e.scalar_tensor_tensor(
                    out=der[:, j], in0=sc, scalar=1.0, in1=gam, op0=ALU.add, op1=ALU.mult),
                    reads=rd, writes=[("der", par, j)])
                P.op(DVE, lambda e, sc=sc, bet=bet, j=j: e.scalar_tensor_tensor(
                    out=der[:, 3 + j], in0=sc, scalar=1.0, in1=bet, op0=ALU.add, op1=ALU.mult),
                    reads=rd, writes=[("der", par, 3 + j)])
                P.op(DVE, lambda e, sh=sh, j=j: e.tensor_tensor(out=der[:, 3 + j], in0=der[:, 3 + j], in1=sh, op=ALU.add),
                     reads=rd + [("der", par, 3 + j)], writes=[("der", par, 3 + j)])
            P.op(DVE, lambda e, gt=gt, coef=coef: e.tensor_scalar(out=gt, in0=gt, scalar1=coef, scalar2=None, op0=ALU.mult),
                 reads=rd, writes=[("gq", par, j)])
        if l == 1 and self.mixers and 1 in js:
            gt = mod[:, 5]
            ps_b = self.poolsT[:].unsqueeze(2).broadcast_to([128, KC, 2])
            P.op(DVE, lambda e: e.tensor_tensor(out=gt, in0=gt, in1=ps_b, op=ALU.mult),
                 reads=[("gq", par, 1), "poolsT"], writes=[("gq", par, 1)])

    def modulate_first(self):
        P = self.P
        for t in range(NT):
            r = 0 if t < 4 else 1
            for k in range(KC):
                P.op(ACT, lambda e, k=k, t=t, r=r: e.activation(
                    out=self.h[:, k, t * TS:(t + 1) * TS], in_=self.x[:, k, t * TS:(t + 1) * TS], func=AF.Identity,
                    scale=self.der[:, 0, 0, k, r:r + 1], bias=self.der[:, 0, 3, k, r:r + 1]),
                    reads=[("x", k, t), ("der", 0, 0), ("der", 0, 3)], writes=[("h", k, t)])

    def phase2(self, l, j, w2slots, n_in, out_chunks, ncols=D, col_of=None, after_tile=None, before_tile=None):
        P = self.P
        par = l % 2
        for t in range(NT):
            r = 0 if t < 4 else 1
            if before_tile is not None:
                before_tile(t)
            for o in out_chunks:
                c0 = (o * 128) if col_of is None else col_of(o)
                b = self.bank()
                first = True
                nmm = sum(n for (_, _, n) in w2slots)
                i = 0
                for (s, a0, n) in w2slots:
                    for c in range(n):
                        i += 1
                        P.op(PE, lambda e, s=s, c=c, a=a0 + c, t=t, b=b, c0=c0, st=first, sp=(i == nmm): e.matmul(
                            self.ps[b][:, :], self.w2v(s, c, ncols, c0, 128), self.abuf[:, a, t * TS:(t + 1) * TS],
                            start=st, stop=sp),
                            reads=[("w2", s), ("a", a0 + c, t)], writes=[("ps", b)])
                        first = False
                xs = self.x[:, o, t * TS:(t + 1) * TS]
                P.op(DVE, lambda e, b=b, xs=xs, o=o, r=r: e.scalar_tensor_tensor(
                    out=xs, in0=self.ps[b][:, :], scalar=self.modT[:, par, 3 * j + 2, o, r:r + 1], in1=xs,
                    op0=ALU.mult, op1=ALU.add),
                    reads=[("ps", b), ("gq", par, j), ("x", o, t)], writes=[("x", o, t)])
            if after_tile is not None:
                after_tile(t)

    def ln_views(self):
        zb, kzb = self.sview(0, (KC, TS), BF16)
        zq, kzq = self.sview(2048, (KC, TS), BF16)
        rs, krs = self.sview(4096, (TS,), F32)
        mn, kmn = self.sview(4608, (TS,), F32)
        return zb, kzb, zq, kzq, rs, krs, mn, kmn

    def ln_stage(self, l, j, last, stage, t):
        P = self.P
        zb, kzb, zq, kzq, rs, krs, mn, kmn = self.ln_views()
        if j < 2:
            nl, nj = l, j + 1
        else:
            nl, nj = l + 1, 0
        npar = nl % 2
        r = 0 if t < 4 else 1
        xt = self.x[:, :, t * TS:(t + 1) * TS]
        xk = [("x", k, t) for k in range(KC)]
        if stage == 1:
            P.op(ACT, lambda e: e.activation(out=zb, in_=xt, func=AF.Copy), reads=xk, writes=kzb)
            P.op(ACT, lambda e: e.activation(out=zq, in_=xt, func=AF.Square), reads=xk, writes=kzq)
        elif stage == 2:
            bm = self.bank()
            bq = self.bank()
            for k in range(KC):
                P.op(PE, lambda e, k=k: e.matmul(self.ps[bm][:, :], self.ones[:], zb[:, k, :], start=(k == 0), stop=(k == KC - 1)),
                     reads=kzb + ["ones"], writes=[("ps", bm)])
            for k in range(KC):
                P.op(PE, lambda e, k=k: e.matmul(self.ps[bq][:, :], self.ones[:], zq[:, k, :], start=(k == 0), stop=(k == KC - 1)),
                     reads=kzq + ["ones"], writes=[("ps", bq)])
            P.op(ACT, lambda e: e.activation(out=mn, in_=self.ps[bm][:, :], func=AF.Copy), reads=[("ps", bm)], writes=kmn)
            P.op(DVE, lambda e: e.tensor_tensor(out=rs, in0=mn, in1=mn, op=ALU.mult), reads=kmn, writes=krs)
            P.op(DVE, lambda e: e.scalar_tensor_tensor(out=rs, in0=self.ps[bq][:, :], scalar=EPS_P, in1=rs,
                                                       op0=ALU.add, op1=ALU.subtract), reads=[("ps", bq)] + krs, writes=krs)
            P.op(ACT, lambda e: e.activation(out=rs, in_=rs, func=AF.Sqrt), reads=krs, writes=krs)
            P.op(DVE, lambda e: e.reciprocal(out=rs, in_=rs), reads=krs, writes=krs)
        elif stage == 3:
            mb = mn.unsqueeze(1).broadcast_to([128, KC, TS])
            rb = rs.unsqueeze(1).broadcast_to([128, KC, TS])
            P.op(DVE, lambda e: e.tensor_tensor(out=xt, in0=xt, in1=mb, op=ALU.subtract), reads=xk + kmn, writes=xk)
            P.op(DVE, lambda e: e.tensor_tensor(out=xt, in0=xt, in1=rb, op=ALU.mult), reads=xk + krs, writes=xk)
        elif stage == 5:
            if not last:
                for k in range(KC):
                    xs = self.x[:, k, t * TS:(t + 1) * TS]
                    P.op(ACT, lambda e, xs=xs, k=k: e.activation(
                        out=self.h[:, k, t * TS:(t + 1) * TS], in_=xs, func=AF.Identity,
                        scale=self.der[:, npar, nj, k, r:r + 1], bias=self.der[:, npar, 3 + nj, k, r:r + 1]),
                        reads=[("x", k, t), ("der", npar, nj), ("der", npar, 3 + nj)], writes=[("h", k, t)])
        else:
            for k in range(KC):
                xs = self.x[:, k, t * TS:(t + 1) * TS]
                if k % 2 == 0:
                    P.op(ACT, lambda e, xs=xs, k=k: e.activation(
                        out=xs, in_=xs, func=AF.Identity, scale=self.lnT[:, l, j, 0, k:k + 1], bias=self.lnT[:, l, j, 1, k:k + 1]),
                        reads=[("x", k, t), "lnT"], writes=[("x", k, t)])
                else:
                    P.op(DVE, lambda e, xs=xs, k=k: e.tensor_scalar(
                        out=xs, in0=xs, scalar1=self.lnT[:, l, j, 0, k:k + 1], scalar2=self.lnT[:, l, j, 1, k:k + 1],
                        op0=ALU.mult, op1=ALU.add), reads=[("x", k, t), "lnT"], writes=[("x", k, t)])

    def ln_cb(self, l, j, last=False):
        order = ((3, 2), (2, 1), (5, 2), (1, 0), (4, 3))

        nxt = self.next_sub
        tile_fn = None
        if nxt is not None and nxt[0] == "ffn" and not last:
            st = self.ffn_head_prepare(nxt[1], nxt[2])
            self.next_head_state = st
            tile_fn = lambda t: self.ffn_head_tile(st, t)
            order = tuple(o for o in order if o[0] != 4)
            self.ln_deferred = [(l, j, last, t) for t in range(NT)]
        elif nxt is not None and nxt[0] == "mix" and self.mixers and nxt[1] % 3 in (0, 2) and not last:
            order = tuple(o for o in order if o[0] != 4)
            self.ln_deferred = [(l, j, last, t) for t in range(NT)]

        def step(i):
            for (stage, lag) in order:
                t = i - lag
                if 0 <= t < NT:
                    self.ln_stage(l, j, last, stage, t)
            if tile_fn is not None and 0 <= i - 3 < NT:
                tile_fn(i - 3)

        def cb(t):
            step(t)
            if t == NT - 1:
                for i in range(NT, NT + 4):
                    self.ln_pending.append(lambda i=i: step(i))
        return cb

    def take_deferred(self):
        if not self.ln_deferred:
            return None
        dfr = {d[3]: d for d in self.ln_deferred}
        self.ln_deferred = []
        return lambda t, dfr=dfr: self.ln_stage(dfr[t][0], dfr[t][1], dfr[t][2], 4, t)

    def ln_step(self):
        if self.ln_pending:
            self.ln_pending.pop(0)()

    def ln_flush(self):
        while self.ln_pending:
            self.ln_pending.pop(0)()

    FFN_GROUPS = [[0, 2], [4, 6], [8, 10], [12, 14], [16], [18, 20]]

    def ffn_unit(self, s, gi, jj, t):
        P = self.P
        hk = [("h", k, t) for k in range(KC)]
        a = gi * 2 + jj
        bg = self.bank()
        bu = self.bank()
        for k in range(KC):
            P.op(PE, lambda e, k=k: e.matmul(
                self.ps[bg][:, :], self.w1v(s, k, jj * 128, 128), self.h[:, k, t * TS:(t + 1) * TS],
                start=(k == 0), stop=(k == KC - 1)), reads=[("w1", s)] + hk, writes=[("ps", bg)])
        for k in range(KC):
            P.op(PE, lambda e, k=k: e.matmul(
                self.ps[bu][:, :], self.w1v(s, k, 256 + jj * 128, 128), self.h[:, k, t * TS:(t + 1) * TS],
                start=(k == 0), stop=(k == KC - 1)), reads=[("w1", s)] + hk, writes=[("ps", bu)])
        sg, ksg = self.sview(5632 + self.sgi * 512, (TS,), F32)
        self.sgi = (self.sgi + 1) % 2
        P.op(ACT, lambda e: e.activation(out=sg, in_=self.ps[bg][:, :], func=AF.Silu), reads=[("ps", bg)], writes=ksg)
        P.op(DVE, lambda e: e.tensor_tensor(out=self.abuf[:, a, t * TS:(t + 1) * TS], in0=self.ps[bu][:, :], in1=sg,
                                            op=ALU.mult), reads=[("ps", bu)] + ksg, writes=[("a", a, t)])

    def ffn_head_prepare(self, l, f):
        wgu = self.d_wgu[l, f].rearrange("(k p) n -> p k n", p=128)
        slots1 = []
        for j0 in self.FFN_GROUPS[0]:
            s, _ = self.load_w1([(wgu[:, :, j0 * 128:(j0 + 2) * 128], 256),
                                 (wgu[:, :, DFF + j0 * 128:DFF + (j0 + 2) * 128], 256)])
            slots1.append(s)
        return {"l": l, "f": f, "slots1": slots1}

    def ffn_head_tile(self, st, t):
        tag = self.P.tag
        self.P.tag = "L%d.ffn%d.head" % (st["l"], st["f"])
        for gi in range(len(self.FFN_GROUPS[0])):
            for jj in range(2):
                self.ffn_unit(st["slots1"][gi], gi, jj, t)
        self.P.tag = tag

    def ffn(self, l, f, last=False, head=None):
        P = self.P
        j = 0 if f == 0 else 2
        wgu = self.d_wgu[l, f].rearrange("(k p) n -> p k n", p=128)
        wdn = self.d_wdn[l, f].rearrange("(c p) n -> p c n", p=128)
        groups = self.FFN_GROUPS
        base_tag = P.tag
        for gno, grp in enumerate(groups):
            P.tag = base_tag + ".g%d.p1" % gno
            slots2 = []
            if gno == 0:
                if head is None:
                    head = self.ffn_head_prepare(l, f)
                    for t in range(NT):
                        self.ffn_head_tile(head, t)
                        self.ln_step()
                self.ln_flush()
                for gi, j0 in enumerate(grp):
                    slots2.append((self.load_w2(wdn[:, j0:j0 + 2, :], 2, D), gi * 2, 2))
            else:
                slots1 = []
                for gi, j0 in enumerate(grp):
                    s, _ = self.load_w1([(wgu[:, :, j0 * 128:(j0 + 2) * 128], 256),
                                         (wgu[:, :, DFF + j0 * 128:DFF + (j0 + 2) * 128], 256)])
                    slots1.append(s)
                    slots2.append((self.load_w2(wdn[:, j0:j0 + 2, :], 2, D), gi * 2, 2))
                for gi in range(len(grp)):
                    for t in range(NT):
                        for jj in range(2):
                            self.ffn_unit(slots1[gi], gi, jj, t)
            P.tag = base_tag + ".g%d.mod" % gno
            is_last = (gno == len(groups) - 1)
            first_ffn = (l == 0 and f == 0)
            self.mod_step(2 if (is_last and first_ffn) else 1)
            if not first_ffn and not is_last:
                self.mod_prefetch()
            if is_last:
                if l == 0 and f == 0:
                    self.derive(0, (1, 2))
                if f == 1 and l + 1 < self.n_layers:
                    self.mod_step(18)
                    self.derive(l + 1)
            P.tag = base_tag + ".g%d.p2" % gno
            bt = self.take_deferred() if gno == 0 else None
            self.phase2(l, j, slots2, len(grp) * 2, range(KC), after_tile=self.ln_cb(l, j, last) if is_last else None,
                        before_tile=bt)
            if not is_last and first_ffn:
                P.tag = base_tag + ".g%d.mod" % gno
                self.mod_step(1)
            elif not is_last and getattr(self, "mod_pending", None):
                P.tag = base_tag + ".g%d.mod" % gno
                self.mod_step(1)
                self.mod_prefetch()
        P.tag = base_tag

    def build(self):
        P, nc = self.P, self.nc
        self.declare()
        self.prologue()
        self.compute_mod(0)
        self.mod_step(6)
        self.derive(0, (0,))
        self.modulate_first()
        nl = self.n_layers
        subs = []
        for l in range(nl):
            subs += [("ffn", l, 0), ("mix", l), ("ffn", l, 1)]
        head = None
        for i, sub in enumerate(subs):
            self.next_sub = subs[i + 1] if i + 1 < len(subs) else None
            self.next_head_state = None
            l = sub[1]
            if sub[0] == "ffn":
                f = sub[2]
                P.tag = "L%d.ffn%d" % (l, f)
                if f == 0:
                    if l + 1 < nl:
                        self.compute_mod(l + 1)
                    self.ffn(l, 0, head=head)
                else:
                    self.ffn(l, 1, last=(l == nl - 1), head=head)
            else:
                P.tag = "L%d.mix" % l
                kind = l % 3
                if self.mixers:
                    if kind == 0:
                        self.conv_mixer(l)
                    elif kind == 1:
                        self.pool_mixer(l)
                    else:
                        self.attn_mixer(l)
                else:
                    cb = self.ln_cb(l, 1)
                    for t in range(NT):
                        cb(t)
            head = self.next_head_state
        self.ln_flush()
        assert not self.ln_deferred
        yv = self.o_yT.rearrange("(k p) t -> p k t", p=128)
        for t in range(NT):
            P.dma(SP, "yout", yv[:, :, t * TS:(t + 1) * TS], self.x[:, :, t * TS:(t + 1) * TS],
                  reads=[("x", k, t) for k in range(KC)])
        outs = ["yout"] + [k for k in ("kvout0", "kvout1") if k in P.dma_cnt]
        P.wait_all_dma(SP, outs)
        P.prepare(nc)
        with nc.Block() as block:
            P.emit(nc, block)
        return nc


    def skeys(self, off_words, nwords):
        return [("scr", c) for c in range(off_words // 128, (off_words + nwords + 127) // 128)]

    def conv_mixer(self, l):
        P = self.P
        self.ln_flush()
        ci = l // 3
        par = l % 2
        win = self.d_cwin[ci].rearrange("(k p) n -> p k n", p=128)
        wout = self.d_cwout[ci].rearrange("(c p) n -> p c n", p=128)
        ub, kub = self.sview(0, (NTOK,), F32)
        cb, kcb = self.sview(2560, (NTOK,), F32)
        vt = [self.sview(5120 + i * 512, (TS,), F32) for i in range(2)]
        vti = 0
        for grp in range(2):
            slots2 = []
            for oo in range(4):
                o = grp * 4 + oo
                s, _ = self.load_w1([(win[:, :, o * 128:(o + 1) * 128], 128),
                                     (win[:, :, D + o * 128:D + (o + 1) * 128], 128),
                                     (win[:, :, 2 * D + o * 128:2 * D + (o + 1) * 128], 128)])
                if oo % 2 == 0:
                    s2 = self.load_w2(wout[:, o:o + 2, :], 2, D)
                    slots2.append((s2, oo, 2))
                for t in range(NT):
                    hk = [("h", k, t) for k in range(KC)]
                    bc = self.bank()
                    bv = self.bank()
                    for (bb, c0) in ((bc, 128), (bv, 256)):
                        for k in range(KC):
                            P.op(PE, lambda e, s=s, k=k, t=t, bb=bb, c0=c0: e.matmul(
                                self.ps[bb][:, :], self.w1v(s, k, c0, 128), self.h[:, k, t * TS:(t + 1) * TS],
                                start=(k == 0), stop=(k == KC - 1)),
                                reads=[("w1", s)] + hk, writes=[("ps", bb)])
                    vv, kvv = vt[vti]
                    vti = (vti + 1) % 2
                    P.op(ACT, lambda e, vv=vv, bv=bv: e.activation(out=vv, in_=self.ps[bv][:, :], func=AF.Copy),
                         reads=[("ps", bv)], writes=kvv)
                    P.op(DVE, lambda e, vv=vv, bc=bc, t=t: e.tensor_tensor(
                        out=ub[:, t * TS:(t + 1) * TS], in0=self.ps[bc][:, :], in1=vv, op=ALU.mult),
                        reads=[("ps", bc)] + kvv, writes=self.skeys(t * TS, TS))
                self.mod_step(1)
                self.mod_prefetch()
                k0 = self.ckT[:, ci, 0, o:o + 1]
                k1 = self.ckT[:, ci, 1, o:o + 1]
                k2 = self.ckT[:, ci, 2, o:o + 1]
                for (s0, e0) in SEGS:
                    P.op(DVE, lambda e, s0=s0, e0=e0, k1=k1: e.tensor_scalar(
                        out=cb[:, s0:e0], in0=ub[:, s0:e0], scalar1=k1, scalar2=None, op0=ALU.mult),
                        reads=kub + ["ckT"], writes=kcb)
                    P.op(DVE, lambda e, s0=s0, e0=e0, k0=k0: e.scalar_tensor_tensor(
                        out=cb[:, s0 + 1:e0], in0=ub[:, s0:e0 - 1], scalar=k0, in1=cb[:, s0 + 1:e0],
                        op0=ALU.mult, op1=ALU.add), reads=kub + kcb + ["ckT"], writes=kcb)
                    P.op(DVE, lambda e, s0=s0, e0=e0, k2=k2: e.scalar_tensor_tensor(
                        out=cb[:, s0:e0 - 1], in0=ub[:, s0 + 1:e0], scalar=k2, in1=cb[:, s0:e0 - 1],
                        op0=ALU.mult, op1=ALU.add), reads=kub + kcb + ["ckT"], writes=kcb)
                for t in range(NT):
                    hk = [("h", k, t) for k in range(KC)]
                    bb = self.bank()
                    for k in range(KC):
                        P.op(PE, lambda e, s=s, k=k, t=t, bb=bb: e.matmul(
                            self.ps[bb][:, :], self.w1v(s, k, 0, 128), self.h[:, k, t * TS:(t + 1) * TS],
                            start=(k == 0), stop=(k == KC - 1)),
                            reads=[("w1", s)] + hk, writes=[("ps", bb)])
                    P.op(DVE, lambda e, bb=bb, t=t, oo=oo: e.tensor_tensor(
                        out=self.abuf[:, oo, t * TS:(t + 1) * TS], in0=self.ps[bb][:, :], in1=cb[:, t * TS:(t + 1) * TS],
                        op=ALU.mult), reads=[("ps", bb)] + kcb, writes=[("a", oo, t)])
            self.phase2(l, 1, slots2, 4, range(KC), after_tile=self.ln_cb(l, 1) if grp == 1 else None,
                        before_tile=self.take_deferred() if grp == 0 else None)

    def pool_mixer(self, l):
        P = self.P
        self.ln_flush()
        par = l % 2
        PADW = 8
        passes = [[(0, LS, 0)], [(LS, LP, 1), (LS + LP, LP, 1)]]
        for g in range(4):
            w = 2 << g
            s2 = self.load_w2(self.d_poolw[g].rearrange("(c p) n -> p c n", p=128), 2, 256)
            for a in range(2):
                kc = 2 * g + a
                for segs in passes:
                    N = sum(ln + 2 * PADW for (_, ln, _) in segs)
                    bufs = [self.sview(i * 2064, (N,), F32) for i in range(3)]
                    hp, khp = bufs[0]
                    off = 0
                    offs = []
                    for (t0, ln, r) in segs:
                        P.op(DVE, lambda e, off=off, hp=hp: e.memset(hp[:, off:off + PADW], 0.0), writes=khp)
                        P.op(DVE, lambda e, off=off, ln=ln, hp=hp: e.memset(hp[:, off + PADW + ln:off + 2 * PADW + ln], 0.0), writes=khp)
                        offs.append(off + PADW)
                        off += ln + 2 * PADW
                    for (t0, ln, r), o0 in zip(segs, offs):
                        P.op(ACT, lambda e, t0=t0, ln=ln, r=r, o0=o0, hp=hp, kc=kc: e.activation(
                            out=hp[:, o0:o0 + ln], in_=self.x[:, kc, t0:t0 + ln], func=AF.Identity,
                            scale=self.der[:, par, 6, kc, r:r + 1], bias=self.modT[:, par, 3, kc, r:r + 1]),
                            reads=[("x", kc, t) for t in range(t0 // TS, (t0 + ln + TS - 1) // TS)] +
                                  [("der", par, 6), ("mod", par)], writes=khp)
                    if segs is passes[1]:
                        self.mod_step(1)
                        self.mod_prefetch()
                    cur, kcur = hp, khp
                    nb = 1
                    for lev in range(1, g + 2):
                        nxt, knxt = bufs[nb]
                        if lev == 1:
                            P.op(DVE, lambda e, cur=cur, nxt=nxt, N=N: e.tensor_tensor(
                                out=nxt[:, 1:N], in0=cur[:, 0:N - 1], in1=cur[:, 1:N], op=ALU.add),
                                reads=kcur, writes=knxt)
                        else:
                            d = 1 << (lev - 2)
                            P.op(DVE, lambda e, cur=cur, nxt=nxt, N=N, d=d: e.tensor_tensor(
                                out=nxt[:, 2 * d:N - 2 * d], in0=cur[:, d:N - 3 * d], in1=cur[:, 3 * d:N - d], op=ALU.add),
                                reads=kcur, writes=knxt)
                        cur, kcur = nxt, knxt
                        nb = 2 if nb == 1 else 1
                    for (t0, ln, r), o0 in zip(segs, offs):
                        P.op(DVE, lambda e, cur=cur, t0=t0, ln=ln, o0=o0, hp=hp, a=a, w=w: e.scalar_tensor_tensor(
                            out=self.abuf[:, a, t0:t0 + ln], in0=cur[:, o0:o0 + ln], scalar=1.0 / w, in1=hp[:, o0:o0 + ln],
                            op0=ALU.mult, op1=ALU.subtract),
                            reads=kcur + khp, writes=[("a", a, t) for t in range(t0 // TS, (t0 + ln + TS - 1) // TS)])
                        tmp, ktmp = self.sview(6192 + 0, (8,), F32)
                        for side in range(2):
                            e0 = o0 if side == 0 else o0 + ln - 8
                            a0 = t0 if side == 0 else t0 + ln - 8
                            P.op(DVE, lambda e, cur=cur, e0=e0, side=side, tmp=tmp, g=g: e.tensor_tensor(
                                out=tmp, in0=cur[:, e0:e0 + 8], in1=self.iedge[:, g, side, :], op=ALU.mult),
                                reads=kcur + ["iedge"], writes=ktmp)
                            P.op(DVE, lambda e, e0=e0, a0=a0, tmp=tmp, hp=hp, a=a: e.tensor_tensor(
                                out=self.abuf[:, a, a0:a0 + 8], in0=tmp, in1=hp[:, e0:e0 + 8], op=ALU.subtract),
                                reads=ktmp + khp, writes=[("a", a, a0 // TS)])
            self.mod_step(1)
            self.mod_prefetch()
            self.phase2(l, 1, [(s2, 0, 2)], 2, [2 * g, 2 * g + 1], ncols=256, col_of=lambda o, g=g: (o - 2 * g) * 128,
                        after_tile=self.ln_cb(l, 1) if g == 3 else None)

    def attn_mixer(self, l):
        P = self.P
        self.ln_flush()
        par = l % 2
        KT_O, VA_O, PT_O, RP_O = 0, 1536, 3840, 4608
        kT = self.scr[:, KT_O:KT_O + 1536].bitcast(BF16)
        va = self.scr[:, VA_O:VA_O + 2304].bitcast(BF16).rearrange("p (t c) -> p t c", t=24)
        kkT = lambda i0, n=1: self.skeys(KT_O + i0 * 64, n * 64)
        kva = lambda i0, n=1: self.skeys(VA_O + i0 * 96, n * 96)
        pts = [(self.scr[:, PT_O + i * 256:PT_O + (i + 1) * 256].bitcast(BF16), self.skeys(PT_O + i * 256, 256))
               for i in range(3)]
        cs = self.scr[:, RP_O:RP_O + 1024].rearrange("p (a n) -> p a n", a=2)
        kcs = self.skeys(RP_O, 1024)
        t1, kt1 = self.sview(RP_O + 1024, (TS,), F32)
        t2, kt2 = self.sview(RP_O + 1536, (TS,), F32)
        rec, krec = self.sview(RP_O, (TS,), F32)
        rec2, krec2 = self.sview(RP_O + 512, (TS,), F32)
        eh = self.scr[0:1, RP_O + 1024:RP_O + 1280].bitcast(BF16)
        keh = self.skeys(RP_O + 1024, 256)
        el = self.scr[0:1, RP_O + 1280:RP_O + 1536].bitcast(BF16)
        kel = self.skeys(RP_O + 1280, 256)
        stg = [self.sview(RP_O + 1024 + i * 512, (TS,), F32) for i in range(2)]
        qb_ = lambda cc: self.abuf[:, 2 + cc, :]

        P.dma(POOL, "trim", self.trim[:], self.d_tri, writes=["trim"])
        P.dma(POOL, "ident", self.ident[:], self.d_ident, writes=["ident"])
        P.dma(SP, "sink", self.es[:], self.d_sink, writes=["es"])
        P.op(ACT, lambda e: e.activation(out=self.es[:], in_=self.es[:], func=AF.Exp), reads=["es"], writes=["es"])
        P.op(DVE, lambda e: e.memset(self.ones1[:], 1.0), writes=["ones1"])
        P.op(DVE, lambda e: e.memset(va, 1.0), writes=kva(0, 24))

        s, _ = self.load_w1([(self.d_wkv.rearrange("(k p) n -> p k n", p=128), 512)])
        for blk in range(4):
            tok0 = LS + blk * 128
            b = self.bank()
            for k in range(KC):
                P.op(PE, lambda e, s=s, k=k, b=b, tok0=tok0: e.matmul(
                    self.ps[b][:, :], self.h[:, k, tok0:tok0 + 128], self.w1v(s, k, 0, 512),
                    start=(k == 0), stop=(k == KC - 1)),
                    reads=[("w1", s)] + [("h", k, 4) for k in range(KC)], writes=[("ps", b)])
            sg, ksg = stg[blk % 2]
            P.op(ACT, lambda e, sg=sg, b=b: e.activation(out=sg, in_=self.ps[b][:, :], func=AF.Copy),
                 reads=[("ps", b)], writes=ksg)
            P.dma(SP, "kvout%d" % (blk % 2), self.o_kv[blk * 128:(blk + 1) * 128, :], sg, reads=ksg)

        acc_i = 0
        sb_i = 0
        pt_i = 0
        for j in range(4):
            wq = self.d_wqk[j].rearrange("(k p) n -> p k n", p=128)
            sA, _ = self.load_w1([(wq[:, :, 0:512], 512)])
            sB, _ = self.load_w1([(wq[:, :, 512:832], 320)])
            s2 = self.load_w2(self.d_wo.rearrange("(c p) n -> p c n", p=128)[:, 2 * j:2 * j + 2, :], 2, D)
            P.dma(POOL, "kctx", kT[:, 2560:3072], self.d_kctx[j], writes=kkT(20, 4))
            P.dma(POOL, "vctx", va[:, 20:24, 64:128], self.d_vctx[:, j], writes=kva(20, 4))
            for t in range(NT):
                hk = [("h", k, t) for k in range(KC)]
                tsl = slice(t * TS, (t + 1) * TS)
                rope = t < 4
                if rope:
                    P.dma(SP, "cs", cs[:, 0, :], self.d_cos[:, tsl], writes=kcs)
                    P.dma(SP, "cs", cs[:, 1, :], self.d_sin[:, tsl], writes=kcs)
                jobs = [(sA, 0, 256, qb_(0)[:, tsl], [("a", 2, t)]), (sA, 128, 384, qb_(1)[:, tsl], [("a", 3, t)]),
                        (sB, 0, 128, kT[:, tsl], kkT(4 * t, 4))]
                for (sw, c_main, c_swap, dst, kdst) in jobs:
                    b1 = self.bank()
                    for k in range(KC):
                        P.op(PE, lambda e, sw=sw, k=k, b1=b1, c_main=c_main, tsl=tsl: e.matmul(
                            self.ps[b1][:, :], self.w1v(sw, k, c_main, 128), self.h[:, k, tsl],
                            start=(k == 0), stop=(k == KC - 1)), reads=[("w1", sw)] + hk, writes=[("ps", b1)])
                    if rope:
                        b2 = self.bank()
                        for k in range(KC):
                            P.op(PE, lambda e, sw=sw, k=k, b2=b2, c_swap=c_swap, tsl=tsl: e.matmul(
                                self.ps[b2][:, :], self.w1v(sw, k, c_swap, 128), self.h[:, k, tsl],
                                start=(k == 0), stop=(k == KC - 1)), reads=[("w1", sw)] + hk, writes=[("ps", b2)])
                        P.op(DVE, lambda e, b1=b1: e.tensor_tensor(out=t1, in0=self.ps[b1][:, :], in1=cs[:, 0, :], op=ALU.mult),
                             reads=[("ps", b1)] + kcs, writes=kt1)
                        P.op(DVE, lambda e, b2=b2: e.tensor_tensor(out=t2, in0=self.ps[b2][:, :], in1=cs[:, 1, :], op=ALU.mult),
                             reads=[("ps", b2)] + kcs, writes=kt2)
                        P.op(DVE, lambda e, dst=dst: e.tensor_tensor(out=dst, in0=t1, in1=t2, op=ALU.add),
                             reads=kt1 + kt2, writes=kdst)
                    else:
                        P.op(ACT, lambda e, dst=dst, b1=b1: e.activation(out=dst, in_=self.ps[b1][:, :], func=AF.Copy),
                             reads=[("ps", b1)], writes=kdst)
                bv = self.bank()
                for blk in range(4):
                    tok0 = t * TS + blk * 128
                    for k in range(KC):
                        P.op(PE, lambda e, k=k, bv=bv, blk=blk, tok0=tok0, sB=sB: e.matmul(
                            self.ps[bv][:, blk * 64:(blk + 1) * 64], self.h[:, k, tok0:tok0 + 128], self.w1v(sB, k, 256, 64),
                            start=(k == 0), stop=(k == KC - 1)), reads=[("w1", sB)] + hk, writes=[("ps", bv)])
                P.op(ACT, lambda e, bv=bv, t=t: e.activation(
                    out=va[:, 4 * t:4 * t + 4, 64:128], in_=self.ps[bv][:, 0:256].rearrange("p (a d) -> p a d", a=4),
                    func=AF.Copy), reads=[("ps", bv)], writes=kva(4 * t, 4))
            self.mod_step(1)
            self.mod_prefetch()
            esb = self.es[0:1, 4 * j:4 * j + 4].unsqueeze(2).broadcast_to([1, 4, 128])
            eh3 = eh.rearrange("p (a q) -> p a q", a=4)
            el3 = el.rearrange("p (a q) -> p a q", a=4)
            P.op(DVE, lambda e, esb=esb: e.tensor_copy(out=eh3, in_=esb), reads=["es"], writes=keh)
            P.op(DVE, lambda e, esb=esb: e.tensor_tensor(out=el3, in0=esb, in1=eh3, op=ALU.subtract),
                 reads=["es"] + keh, writes=kel)
            blocks = []
            for qb in range(16):
                kts = [(kb, (0 if kb == qb - 1 else (1 if kb == qb + 1 else None)))
                       for kb in (qb - 1, qb, qb + 1) if 0 <= kb < 16]
                kts += [(20 + i, None) for i in range(4)]
                blocks.append((qb * 128, kts))
            for sq in range(2):
                for pb in range(2):
                    blocks.append((LS + sq * LP + pb * 128, [(16 + sq * 2 + i, None) for i in range(2)]))
            units = []
            for bi, (q0, kts) in enumerate(blocks):
                for ki, (kt_, msk) in enumerate(kts):
                    units.append((bi, q0, ki, len(kts), kt_, msk))
            LAG = 2
            state = {}

            def front(u, ui):
                (bi, q0, ki, nk, kt_, msk) = u
                tq = q0 // TS
                qk = [("a", 2, tq), ("a", 3, tq)]
                bs2 = (4, 5) if ui % 2 == 0 else (6, 7)
                pt, kpt = pts[ui % 3]
                kcols = slice(kt_ * 128, (kt_ + 1) * 128)
                for half in range(2):
                    p0 = half * 64
                    bs_ = bs2[half]
                    P.op(PE, lambda e, bs_=bs_, p0=p0, kcols=kcols, q0=q0, msk=msk: e.matmul(
                        self.ps[bs_][:, 0:256], kT[p0:p0 + 64, kcols],
                        self.abuf[p0:p0 + 64, 2:4, q0:q0 + 128], start=True, stop=(msk is None)),
                        reads=kkT(kt_) + qk, writes=[("ps", bs_)])
                if msk is not None:
                    for half in range(2):
                        bs_ = bs2[half]
                        P.op(PE, lambda e, bs_=bs_, msk=msk: e.matmul(
                            self.ps[bs_][:, 0:256], self.ident[:], self.trim[:, msk, :], start=False, stop=True),
                            reads=["ident", "trim"], writes=[("ps", bs_)])
                for half in range(2):
                    bs_ = bs2[half]
                    P.op(ACT, lambda e, pt=pt, bs_=bs_, half=half: e.activation(
                        out=pt[:, half * 256:(half + 1) * 256], in_=self.ps[bs_][:, 0:256], func=AF.Exp, scale=ATT_SCALE),
                        reads=[("ps", bs_)], writes=kpt)

            def back(u, ui):
                (bi, q0, ki, nk, kt_, msk) = u
                tq = q0 // TS
                bn, bd = (0, 1) if bi % 2 == 0 else (2, 3)
                pt, kpt = pts[ui % 3]
                first = (ki == 0)
                last = (ki == nk - 1)
                P.op(PE, lambda e: e.matmul(self.ps[bn][:, 0:256], va[:, kt_, 64:192], pt[:, 0:256],
                                            start=first, stop=last, skip_group_check=True),
                     reads=kva(kt_) + kpt, writes=[("ps", bn)])
                P.op(PE, lambda e: e.matmul(self.ps[bn][:, 256:512], va[:, kt_, 0:128], pt[:, 256:512],
                                            start=False, stop=last, skip_group_check=True),
                     reads=kva(kt_) + kpt, writes=[("ps", bn)])
                P.op(PE, lambda e: e.matmul(self.ps[bd][:, :], self.ones1[:], pt[:, :], start=first, stop=False),
                     reads=["ones1"] + kpt, writes=[("ps", bd)])
                if not last:
                    return
                P.op(PE, lambda e: e.matmul(self.ps[bd][:, :], self.ones1[0:1, :], eh[0:1, :], start=False, stop=False),
                     reads=["ones1"] + keh, writes=[("ps", bd)])
                P.op(PE, lambda e: e.matmul(self.ps[bd][:, :], self.ones1[0:1, :], el[0:1, :], start=False, stop=True),
                     reads=["ones1"] + kel, writes=[("ps", bd)])
                P.op(DVE, lambda e: e.reciprocal(out=rec, in_=self.ps[bd][:, :]), reads=[("ps", bd)], writes=krec)
                for (nlo, c0) in ((0, 0), (64, 256)):
                    P.op(DVE, lambda e, nlo=nlo, c0=c0: e.tensor_tensor(
                        out=self.abuf[nlo:nlo + 64, 0:2, q0:q0 + 128],
                        in0=self.ps[bn][nlo:nlo + 64, c0:c0 + 256].rearrange("p (a q) -> p a q", a=2),
                        in1=rec[nlo:nlo + 64, c0:c0 + 256].rearrange("p (a q) -> p a q", a=2), op=ALU.mult),
                        reads=[("ps", bn)] + krec, writes=[("a", 0, tq), ("a", 1, tq)])

            for ui in range(len(units) + LAG):
                if ui < len(units):
                    front(units[ui], ui)
                if ui >= LAG:
                    back(units[ui - LAG], ui - LAG)
            self.mod_step(1)
            self.mod_prefetch()
            self.phase2(l, 1, [(s2, 0, 2)], 2, range(KC), after_tile=self.ln_cb(l, 1) if j == 3 else None,
                        before_tile=self.take_deferred() if j == 0 else None)


def _host_inputs(x_prompt, x_sample, cache_ctx_k, cache_ctx_v, c, c_ctx, w_mod, b_mod, ln_g, ln_b,
                 ffn_w_gate_up, ffn_w_down, conv_w_in, conv_k, conv_w_out, pool_w, pool_scale,
                 attn_w_qkv, attn_w_o, attn_sink, cores=range(NCORES)):
    f = lambda a: np.ascontiguousarray(np.asarray(a, np.float32))
    x_prompt, x_sample = f(x_prompt), f(x_sample)
    cache_ctx_k, cache_ctx_v = f(cache_ctx_k), f(cache_ctx_v)
    c, c_ctx = f(c), f(c_ctx)
    wq = f(attn_w_qkv)[0]
    q_w, k_w, v_w = wq[:, :1024], wq[:, 1024:1280], wq[:, 1280:1536]
    wqk = np.zeros((4, D, 832), np.float32)
    for j in range(4):
        qc = q_w[:, j * 256:(j + 1) * 256]
        kj = k_w[:, j * 64:(j + 1) * 64]
        kd = np.concatenate([kj, kj], axis=1)
        wqk[j] = np.concatenate([qc, _swap32(qc), kd, _swap32(kd), v_w[:, j * 64:(j + 1) * 64]], axis=1)
    cosT, sinT = _rope_tables()
    tri = np.zeros((128, 2, 2, 128), np.float32)
    rr, cc = np.arange(128)[:, None], np.arange(128)[None, :]
    tri[:, 0, :, :] = np.where(rr >= cc, 0.0, -30000.0)[:, None, :]
    tri[:, 1, :, :] = np.where(rr <= cc, 0.0, -30000.0)[:, None, :]
    tri = tri.reshape(128, 2, 256)
    sink = f(attn_sink)[0]
    sinkp = np.zeros((4, 2, 2), np.float32)
    for j in range(4):
        for par in range(2):
            for cc_ in range(2):
                sinkp[j, par, cc_] = sink[4 * j + 2 * cc_ + par]
    shared = {
        "w_mod": f(w_mod), "bmodT": np.ascontiguousarray(np.moveaxis(f(b_mod).reshape(DEPTH, 72, 128), -1, 0)),
        "lnT": np.ascontiguousarray(np.stack([_fm(ln_g), _fm(ln_b)], axis=3)),
        "w_gu": f(ffn_w_gate_up), "w_dn": f(ffn_w_down),
        "conv_w_in": f(conv_w_in), "convkT": _fm(conv_k), "conv_w_out": f(conv_w_out),
        "pool_w": f(pool_w)[0], "poolsT": _fm(f(pool_scale)[0]), "iedge": _pool_inv_counts(),
        "wqk": wqk, "wkv": np.ascontiguousarray(wq[:, 1024:1536]), "w_o": f(attn_w_o)[0],
        "sinkp": sinkp.reshape(1, 16), "cosT": cosT, "sinT": sinT, "trim": tri, "ident": np.eye(128, dtype=np.float32),
    }
    in_maps = []
    for b in cores:
        xT = np.concatenate([x_sample[b].T, x_prompt[2 * b].T, x_prompt[2 * b + 1].T], axis=1)
        cT = np.stack([_fm(c[b]), _fm(c_ctx)], axis=-1)
        ck = cache_ctx_k[b, 0]
        kT = np.transpose(ck, (1, 2, 0))
        kctxT = np.concatenate([kT, kT], axis=1)
        cv = cache_ctx_v[b, 0].reshape(4, 128, 4, HD)
        vctx = np.transpose(cv, (1, 2, 0, 3))
        m = dict(shared)
        m.update({"xT": np.ascontiguousarray(xT), "cT": np.ascontiguousarray(cT),
                  "kctxT": np.ascontiguousarray(kctxT), "vctx": np.ascontiguousarray(vctx)})
        in_maps.append(m)
    return in_maps


_NC_CACHE = {}


def _get_nc(n_layers=DEPTH, mixers=True):
    key = (n_layers, mixers)
    if key not in _NC_CACHE:
        _NC_CACHE[key] = Builder(n_layers, mixers).build()
    return _NC_CACHE[key]


def kernel(**inputs):
    in_maps = _host_inputs(**inputs)
    nc = _get_nc()
    res = run_bass_kernel_spmd(nc, in_maps, core_ids=list(range(NCORES)))
    B, S = 16, LP
    y_p = np.zeros((B, S, D), np.float32)
    y_s = np.zeros((NCORES, LS, D), np.float32)
    ctx_k = np.zeros((B, 1, S, 4, HD), np.float32)
    ctx_v = np.zeros((B, 1, S, 4, HD), np.float32)
    for b in range(NCORES):
        r = res.results[b]
        yT = np.asarray(r["yT"])
        y_s[b] = yT[:, :LS].T
        y_p[2 * b] = yT[:, LS:LS + LP].T
        y_p[2 * b + 1] = yT[:, LS + LP:].T
        kv = np.asarray(r["kvout"])
        for s in range(2):
            ctx_k[2 * b + s, 0] = kv[s * LP:(s + 1) * LP, 0:256].reshape(LP, 4, HD)
            ctx_v[2 * b + s, 0] = kv[s * LP:(s + 1) * LP, 256:512].reshape(LP, 4, HD)
    return (y_p, y_s, ctx_k, ctx_v)
```
